# Optimizing a Trainium2 kernel written in Bass

```python
import jax, jax.numpy as jnp
from jax import lax
import numpy as np

D_MODEL = 1024
BATCH = 8
SEQ = 2048
DEPTH = 2

RET_HEADS = 4
RET_DK = 128
RET_DV = 128
RET_CHUNK = 128
ROPE_BASE = 10000.0
GDN_HEADS = 4
GDN_DK = 128
GDN_DV = 128
GDN_CHUNK = 64
SHORT_CONV = 4
RET_WIDTH = RET_HEADS * RET_DV
GDN_WIDTH = GDN_HEADS * GDN_DV
MIX0_OUT = RET_WIDTH + GDN_WIDTH
MIX0_SPLITS = (RET_HEADS * RET_DK, RET_HEADS * RET_DK, RET_WIDTH, RET_WIDTH,
               GDN_HEADS * GDN_DK, GDN_HEADS * GDN_DK, GDN_WIDTH, GDN_WIDTH,
               GDN_HEADS, GDN_HEADS)
MIX0_IN = sum(MIX0_SPLITS)
D_RNN = D_MODEL
LRU_BLOCKS = 8
LRU_BLOCK = D_RNN // LRU_BLOCKS
LRU_CONV = 4
LRU_C = 8.0
D_FF = ((8 * D_MODEL // 3 + 127) // 128) * 128
FFN_CONV = 3
EPS = 1e-6
N_EVEN = (DEPTH + 1) // 2
N_ODD = DEPTH // 2

kernel_name = "hybrid_retention_gdn_rglru_convffn"


def _rms(x):
    xf = x.astype(jnp.float32)
    return xf * lax.rsqrt(jnp.mean(xf * xf, axis=-1, keepdims=True) + EPS)


def rms_norm(x, gain):
    return (_rms(x) * gain.astype(jnp.float32)).astype(x.dtype)


def l2norm(x):
    return x * lax.rsqrt(jnp.sum(x * x, axis=-1, keepdims=True) + EPS)


def causal_dwconv(x, w):
    width, ch = w.shape
    return lax.conv_general_dilated(
        x, w[:, None, :].astype(x.dtype), window_strides=(1,),
        padding=[(width - 1, 0)], dimension_numbers=('NWC', 'WIO', 'NWC'),
        feature_group_count=ch)


def rotary(x, pos):
    half = x.shape[-1] // 2
    inv_freq = ROPE_BASE ** (-jnp.arange(half, dtype=jnp.float32) / half)
    ang = pos.astype(jnp.float32)[:, None] * inv_freq[None, :]
    cos = jnp.cos(ang)[None, :, None, :]
    sin = jnp.sin(ang)[None, :, None, :]
    x1, x2 = x[..., :half], x[..., half:]
    return jnp.concatenate([x1 * cos - x2 * sin, x1 * sin + x2 * cos], axis=-1)


def to_chunks(x, c):
    b, t, h, d = x.shape
    return x.reshape(b, t // c, c, h, d).transpose(0, 3, 1, 2, 4)


def from_chunks(x):
    b, h, n, c, d = x.shape
    return x.transpose(0, 2, 3, 1, 4).reshape(b, n * c, h, d)


def retention_chunkwise(q, k, v):
    b, t, h, dk = q.shape
    dv = v.shape[-1]
    c = RET_CHUNK
    log_gamma = jnp.log1p(-jnp.exp2(-5.0 - jnp.arange(h, dtype=jnp.float32)))
    qc, kc, vc = to_chunks(q, c), to_chunks(k * dk ** -0.5, c), to_chunks(v, c)
    idx = jnp.arange(c, dtype=jnp.float32)
    rel = idx[:, None] - idx[None, :]
    causal = rel >= 0
    dmask = jnp.where(causal, jnp.exp(log_gamma[:, None, None] * jnp.where(causal, rel, 0.0)), 0.0)
    scores = jnp.einsum('bhnid,bhnjd->bhnij', qc, kc) * dmask[:, None]
    intra = jnp.einsum('bhnij,bhnjv->bhniv', scores, vc)
    k_tail = kc * jnp.exp(log_gamma[:, None, None] * (c - 1 - idx))[..., None]
    kv = jnp.einsum('bhncd,bhncv->nbhdv', k_tail, vc)
    chunk_decay = jnp.exp(log_gamma * c)[None, :, None, None]

    def step(s, kv_n):
        return s * chunk_decay + kv_n, s

    _, s_prev = lax.scan(step, jnp.zeros((b, h, dk, dv), jnp.float32), kv)
    q_decay = qc * jnp.exp(log_gamma[:, None, None] * (idx + 1.0))[..., None]
    inter = jnp.einsum('bhncd,nbhdv->bhncv', q_decay, s_prev)
    return from_chunks(intra + inter)


def gated_delta_chunkwise(q, k, v, g, beta):
    b, t, h, dk = q.shape
    dv = v.shape[-1]
    c = GDN_CHUNK
    qc = to_chunks(q * dk ** -0.5, c)
    kc = to_chunks(k, c)
    vc = to_chunks(v, c)
    gc = jnp.cumsum(to_chunks(g[..., None], c)[..., 0], axis=-1)
    bc = to_chunks(beta[..., None], c)
    tril = jnp.tril(jnp.ones((c, c), bool))
    strict = jnp.tril(jnp.ones((c, c), bool), -1)
    decay = jnp.exp(jnp.where(tril, gc[..., :, None] - gc[..., None, :], -jnp.inf))
    k_beta = kc * bc
    lmat = jnp.where(strict, jnp.einsum('bhnid,bhnjd->bhnij', k_beta, kc) * decay, 0.0)
    rhs = jnp.concatenate([vc * bc, k_beta * jnp.exp(gc)[..., None]], axis=-1)
    sol = lax.linalg.triangular_solve(lmat + jnp.eye(c, dtype=jnp.float32), rhs,
                                      left_side=True, lower=True)
    u, w = sol[..., :dv], sol[..., dv:]
    attn = jnp.where(tril, jnp.einsum('bhnid,bhnjd->bhnij', qc, kc) * decay, 0.0)
    q_decay = qc * jnp.exp(gc)[..., None]
    g_last = gc[..., -1:]
    k_tail = kc * jnp.exp(g_last - gc)[..., None]
    chunk_decay = jnp.exp(g_last)[..., None]
    xs = tuple(jnp.moveaxis(a, 2, 0) for a in (u, w, attn, q_decay, k_tail, chunk_decay))

    def step(s, inp):
        u_n, w_n, a_n, qd_n, kt_n, cd_n = inp
        v_new = u_n - jnp.einsum('bhcd,bhdv->bhcv', w_n, s)
        o = jnp.einsum('bhcd,bhdv->bhcv', qd_n, s) + jnp.einsum('bhij,bhjv->bhiv', a_n, v_new)
        s = s * cd_n + jnp.einsum('bhcd,bhcv->bhdv', kt_n, v_new)
        return s, o

    _, o = lax.scan(step, jnp.zeros((b, h, dk, dv), jnp.float32), xs)
    return from_chunks(jnp.moveaxis(o, 0, 2))


def retention_deltanet_mixer(hn, pos, w_in, conv_w, a_log, dt_bias, out_gain, w_out):
    b, t, _ = hn.shape
    f32 = jnp.float32
    split_at = np.cumsum(MIX0_SPLITS)[:-1].tolist()
    q_r, k_r, v_r, g_r, q_d, k_d, v_d, g_d, b_d, a_d = jnp.split(hn @ w_in, split_at, axis=-1)
    q_r = rotary(q_r.reshape(b, t, RET_HEADS, RET_DK).astype(f32), pos)
    k_r = rotary(k_r.reshape(b, t, RET_HEADS, RET_DK).astype(f32), pos)
    v_r = v_r.reshape(b, t, RET_HEADS, RET_DV).astype(f32)
    y_r = _rms(retention_chunkwise(q_r, k_r, v_r)).reshape(b, t, RET_WIDTH)
    y_r = y_r * jax.nn.silu(g_r.astype(f32))
    qkv = jax.nn.silu(causal_dwconv(jnp.concatenate([q_d, k_d, v_d], axis=-1), conv_w)).astype(f32)
    q_d, k_d, v_d = jnp.split(qkv, [GDN_HEADS * GDN_DK, 2 * GDN_HEADS * GDN_DK], axis=-1)
    q_d = l2norm(q_d.reshape(b, t, GDN_HEADS, GDN_DK))
    k_d = l2norm(k_d.reshape(b, t, GDN_HEADS, GDN_DK))
    v_d = v_d.reshape(b, t, GDN_HEADS, GDN_DV)
    beta = jax.nn.sigmoid(b_d.astype(f32))
    g = -jnp.exp(a_log.astype(f32)) * jax.nn.softplus(a_d.astype(f32) + dt_bias.astype(f32))
    y_d = gated_delta_chunkwise(q_d, k_d, v_d, g, beta)
    y_d = rms_norm(y_d, out_gain) * jax.nn.silu(g_d.astype(f32).reshape(b, t, GDN_HEADS, GDN_DV))
    y = jnp.concatenate([y_r, y_d.reshape(b, t, GDN_WIDTH)], axis=-1).astype(hn.dtype)
    return y @ w_out


def _linear_combine(e1, e2):
    a1, b1 = e1
    a2, b2 = e2
    return a1 * a2, a2 * b1 + b2


def rglru_mixer(hn, w_in, conv_w, conv_b, w_a, b_a, w_x, b_x, lam, w_out):
    b, t, _ = hn.shape
    f32 = jnp.float32
    gate, xr = jnp.split(hn @ w_in, 2, axis=-1)
    xr = (causal_dwconv(xr, conv_w) + conv_b).astype(f32)
    xb = xr.reshape(b, t, LRU_BLOCKS, LRU_BLOCK)
    r = jax.nn.sigmoid(jnp.einsum('btni,nij->btnj', xb, w_a.astype(f32)).reshape(b, t, D_RNN) + b_a)
    i = jax.nn.sigmoid(jnp.einsum('btni,nij->btnj', xb, w_x.astype(f32)).reshape(b, t, D_RNN) + b_x)
    log_a = -LRU_C * r * jax.nn.softplus(-lam.astype(f32))
    a = jnp.exp(log_a)
    u = jnp.sqrt(-jnp.expm1(2.0 * log_a)) * (i * xr)
    _, hs = lax.associative_scan(_linear_combine, (a, u), axis=1)
    y = jax.nn.gelu(gate.astype(f32)) * hs
    return y.astype(hn.dtype) @ w_out


def conv_ffn(hn, w_up, conv_w, conv_b, w_down):
    up = causal_dwconv(hn @ w_up, conv_w) + conv_b
    gate, val = jnp.split(up, 2, axis=-1)
    return (jax.nn.silu(gate) * val) @ w_down


def setup_inputs(seed: int = 0) -> dict:
    key = jax.random.key(seed)
    ks = jax.random.split(key, 24)
    nrm = jax.random.normal
    f32 = jnp.float32
    x = nrm(ks[0], (BATCH, SEQ, D_MODEL), f32)
    norm_mix = 1.0 + 0.02 * nrm(ks[1], (DEPTH, D_MODEL), f32)
    norm_ffn = 1.0 + 0.02 * nrm(ks[2], (DEPTH, D_MODEL), f32)
    ret_gdn_w_in = nrm(ks[3], (N_EVEN, D_MODEL, MIX0_IN), f32) * D_MODEL ** -0.5
    gdn_conv_w = nrm(ks[4], (N_EVEN, SHORT_CONV, 2 * GDN_HEADS * GDN_DK + GDN_WIDTH), f32) * SHORT_CONV ** -0.5
    gdn_a_log = jnp.log(jax.random.uniform(ks[5], (N_EVEN, GDN_HEADS), f32, 1.0, 16.0))
    dt = jnp.exp(jax.random.uniform(ks[6], (N_EVEN, GDN_HEADS), f32, np.log(1e-3), np.log(1e-1)))
    gdn_dt_bias = dt + jnp.log(-jnp.expm1(-dt))
    gdn_out_gain = 1.0 + 0.02 * nrm(ks[7], (N_EVEN, GDN_DV), f32)
    ret_gdn_w_out = nrm(ks[8], (N_EVEN, MIX0_OUT, D_MODEL), f32) * MIX0_OUT ** -0.5
    lru_w_in = nrm(ks[9], (N_ODD, D_MODEL, 2 * D_RNN), f32) * D_MODEL ** -0.5
    lru_conv_w = nrm(ks[10], (N_ODD, LRU_CONV, D_RNN), f32) * LRU_CONV ** -0.5
    lru_conv_b = 0.01 * nrm(ks[11], (N_ODD, D_RNN), f32)
    lru_w_a = nrm(ks[12], (N_ODD, LRU_BLOCKS, LRU_BLOCK, LRU_BLOCK), f32) * LRU_BLOCK ** -0.5
    lru_b_a = 0.01 * nrm(ks[13], (N_ODD, D_RNN), f32)
    lru_w_x = nrm(ks[14], (N_ODD, LRU_BLOCKS, LRU_BLOCK, LRU_BLOCK), f32) * LRU_BLOCK ** -0.5
    lru_b_x = 0.01 * nrm(ks[15], (N_ODD, D_RNN), f32)
    a_c = jax.random.uniform(ks[16], (N_ODD, D_RNN), f32, 0.9, 0.999)
    a0 = a_c ** (1.0 / LRU_C)
    lru_lambda = jnp.log(a0) - jnp.log1p(-a0)
    lru_w_out = nrm(ks[17], (N_ODD, D_RNN, D_MODEL), f32) * D_RNN ** -0.5
    ffn_w_up = nrm(ks[18], (DEPTH, D_MODEL, 2 * D_FF), f32) * D_MODEL ** -0.5
    ffn_conv_w = nrm(ks[19], (DEPTH, FFN_CONV, 2 * D_FF), f32) * FFN_CONV ** -0.5
    ffn_conv_b = 0.01 * nrm(ks[20], (DEPTH, 2 * D_FF), f32)
    ffn_w_down = nrm(ks[21], (DEPTH, D_FF, D_MODEL), f32) * D_FF ** -0.5
    norm_final = 1.0 + 0.02 * nrm(ks[22], (D_MODEL,), f32)
    return {"x": x, "norm_mix": norm_mix, "norm_ffn": norm_ffn,
            "ret_gdn_w_in": ret_gdn_w_in, "gdn_conv_w": gdn_conv_w, "gdn_a_log": gdn_a_log,
            "gdn_dt_bias": gdn_dt_bias, "gdn_out_gain": gdn_out_gain, "ret_gdn_w_out": ret_gdn_w_out,
            "lru_w_in": lru_w_in, "lru_conv_w": lru_conv_w, "lru_conv_b": lru_conv_b,
            "lru_w_a": lru_w_a, "lru_b_a": lru_b_a, "lru_w_x": lru_w_x, "lru_b_x": lru_b_x,
            "lru_lambda": lru_lambda, "lru_w_out": lru_w_out,
            "ffn_w_up": ffn_w_up, "ffn_conv_w": ffn_conv_w, "ffn_conv_b": ffn_conv_b,
            "ffn_w_down": ffn_w_down, "norm_final": norm_final}


def reference(x, norm_mix, norm_ffn, ret_gdn_w_in, gdn_conv_w, gdn_a_log, gdn_dt_bias,
              gdn_out_gain, ret_gdn_w_out, lru_w_in, lru_conv_w, lru_conv_b, lru_w_a, lru_b_a,
              lru_w_x, lru_b_x, lru_lambda, lru_w_out, ffn_w_up, ffn_conv_w, ffn_conv_b,
              ffn_w_down, norm_final):
    pos = jnp.arange(x.shape[1], dtype=jnp.int32)
    h = x
    for layer in range(DEPTH):
        hn = rms_norm(h, norm_mix[layer])
        if layer % 2 == 0:
            e = layer // 2
            h = h + retention_deltanet_mixer(hn, pos, ret_gdn_w_in[e], gdn_conv_w[e], gdn_a_log[e],
                                             gdn_dt_bias[e], gdn_out_gain[e], ret_gdn_w_out[e])
        else:
            o = layer // 2
            h = h + rglru_mixer(hn, lru_w_in[o], lru_conv_w[o], lru_conv_b[o], lru_w_a[o], lru_b_a[o],
                                lru_w_x[o], lru_b_x[o], lru_lambda[o], lru_w_out[o])
        h = h + conv_ffn(rms_norm(h, norm_ffn[layer]), ffn_w_up[layer], ffn_conv_w[layer],
                         ffn_conv_b[layer], ffn_w_down[layer])
    return rms_norm(h, norm_final)
```

```python
import numpy as np
import concourse.bass as bass
import concourse.mybir as mybir
from concourse.bass_utils import run_bass_kernel_spmd

F32 = mybir.dt.float32
BF16 = mybir.dt.bfloat16
AF = mybir.ActivationFunctionType
ALU = mybir.AluOpType
AX = mybir.AxisListType

ENG_ATTR = {"sync": "sync", "act": "scalar", "pool": "gpsimd", "dve": "vector", "pe": "tensor"}
ESIZE = {F32: 4, BF16: 2}
kindof = {}


def _esize(dt):
    return ESIZE[dt]


class Prog:
    def __init__(self, nc, n_dma_sems=40):
        self.nc = nc
        self.ops = []
        self.hist = {}
        self.base = {}
        self.n_dma_sems = n_dma_sems
        self.sb_top = 0

    def sb(self, name, shape, dtype, at=None):
        nbytes = int(np.prod(shape[1:])) * _esize(dtype)
        if at is None:
            at = (self.sb_top + 63) // 64 * 64
            self.sb_top = at + nbytes
        assert at + nbytes <= 229344 - 16512, (name, at, nbytes)
        t = self.nc.alloc_sbuf_tensor_at(name, list(shape), dtype, offset=16512 + at)
        self.base[t.name] = ("sb", at)
        return t

    def ps(self, name, shape, dtype):
        t = self.nc.alloc_psum_tensor(name, list(shape), dtype)
        self.base[t.name] = ("ps:" + name, 0)
        return t

    def dram(self, name, shape, dtype, kind):
        t = self.nc.dram_tensor(name, list(shape), dtype, kind=kind)
        self.base[t.name] = ("dr:" + name, 0)
        kindof[t.name] = kind
        return t

    def region(self, ap):
        t = ap.tensor
        name = t.name
        space, base = self.base[name]
        es = _esize(ap.dtype)
        if space.startswith("dr:"):
            if kindof.get(name) != "ExternalOutput":
                return (space, 0, 1, 0, 1)
            lo = hi = 0
            for st_, cnt_ in ap.ap:
                if st_ >= 0:
                    hi += st_ * (cnt_ - 1)
                else:
                    lo += st_ * (cnt_ - 1)
            return (space, 0, 1, (ap.offset + lo) * es, (ap.offset + hi + 1) * es)
        pstride = int(np.prod(t.shape[1:]))
        tes = _esize(t.dtype)
        off = ap.offset
        dims = list(ap.ap)
        p0 = off // pstride
        inoff = off % pstride
        pst, pcnt = dims[0]
        assert pst == pstride or pcnt == 1, (name, dims, pstride)
        lo = hi = 0
        for st, cnt in dims[1:]:
            if st >= 0:
                hi += st * (cnt - 1)
            else:
                lo += st * (cnt - 1)
        a0 = base + (inoff + lo) * es
        a1 = base + (inoff + hi + 1) * es
        return (space, p0, p0 + pcnt, a0, a1)

    def regions(self, ap):
        reg = self.region(ap)
        space = reg[0]
        if space != "sb":
            return [reg]
        free = [(st, cnt) for st, cnt in list(ap.ap)[1:] if cnt > 1]
        if len(free) < 2:
            return [reg]
        k = max(range(len(free)), key=lambda i: abs(free[i][0]))
        st0, cnt0 = free[k]
        inner = 1 + sum(abs(st) * (cnt - 1) for i, (st, cnt) in enumerate(free) if i != k)
        if st0 <= 0 or st0 < inner or cnt0 > 32:
            return [reg]
        es = _esize(ap.dtype)
        _, p0, p1, a0, _ = reg
        return [(space, p0, p1, a0 + j * st0 * es, a0 + j * st0 * es + inner * es) for j in range(cnt0)]

    PAGE = 2048

    def _conflicts(self, reg, is_write, opid, deps, eng, dma):
        space, p0, p1, a0, a1 = reg
        if space.startswith("ps:"):
            stt = self.hist.setdefault(space, {})
            other = [oid for e2, oid in stt.items() if e2 != eng]
            if other:
                deps.update(other)
                stt.clear()
            stt[eng] = opid
            return
        pages = self.hist.setdefault(space, {})
        ent_new = (p0, p1, a0, a1, is_write, opid)
        for pg in range(a0 // self.PAGE, (a1 - 1) // self.PAGE + 1):
            lst = pages.get(pg)
            if lst is None:
                pages[pg] = [ent_new]
                continue
            keep = []
            for ent in lst:
                q0, q1, b0, b1, w, oid = ent
                if oid == opid:
                    keep.append(ent)
                    continue
                if (not is_write and not w and not dma and q0 == p0 and q1 == p1 and b0 == a0 and b1 == a1
                        and self.ops[oid]["eng"] == eng and not self.ops[oid]["dma"]):
                    continue
                overlap = not (q1 <= p0 or p1 <= q0 or b1 <= a0 or a1 <= b0)
                if overlap and (w or is_write):
                    deps.add(oid)
                if is_write and q0 >= p0 and q1 <= p1 and b0 >= a0 and b1 <= a1:
                    continue
                keep.append(ent)
            keep.append(ent_new)
            pages[pg] = keep

    def add(self, eng, fn, reads=(), writes=(), dma=False):
        opid = len(self.ops)
        deps = set()
        self.ops.append(dict(eng=eng, fn=fn, deps=deps, dma=dma, id=opid))
        for ap in writes:
            for reg in self.regions(ap):
                self._conflicts(reg, True, opid, deps, eng, dma)
        for ap in reads:
            for reg in self.regions(ap):
                self._conflicts(reg, False, opid, deps, eng, dma)
        return opid

    def dma(self, out, in_, eng="sync"):
        return self.add(eng, lambda e: e.dma_start(out=out, in_=in_), [in_], [out], dma=True)

    def mm(self, out, lhsT, rhs, start=True, stop=True):
        return self.add("pe", lambda e: e.matmul(out, lhsT, rhs, start=start, stop=stop), [lhsT, rhs], [out])

    def tr(self, out, in_, ident):
        return self.add("pe", lambda e: e.transpose(out, in_, ident), [in_, ident], [out])

    def act(self, out, in_, func, bias=None, scale=None, eng="act"):
        kw = {}
        rd = [in_]
        if bias is not None:
            kw["bias"] = bias
            if not isinstance(bias, (int, float)):
                rd.append(bias)
        if scale is not None:
            kw["scale"] = scale
            if not isinstance(scale, (int, float)):
                rd.append(scale)
        return self.add(eng, lambda e: e.activation(out, in_, func, **kw), rd, [out])

    def tt(self, eng, out, in0, in1, op):
        return self.add(eng, lambda e: e.tensor_tensor(out, in0, in1, op), [in0, in1], [out])

    def ts(self, eng, out, in0, s1, op0, s2=None, op1=None):
        rd = [in0] + [s for s in (s1, s2) if s is not None and not isinstance(s, (int, float))]
        if op1 is None:
            return self.add(eng, lambda e: e.tensor_scalar(out, in0, s1, None, op0), rd, [out])
        return self.add(eng, lambda e: e.tensor_scalar(out, in0, s1, s2, op0, op1), rd, [out])

    def stt(self, eng, out, in0, scalar, in1, op0, op1):
        rd = [in0, in1] + ([] if isinstance(scalar, (int, float)) else [scalar])
        return self.add(eng, lambda e: e.scalar_tensor_tensor(out, in0, scalar, in1, op0, op1), rd, [out])

    def copy(self, eng, out, in_):
        if eng == "act":
            return self.add(eng, lambda e: e.copy(out, in_), [in_], [out])
        return self.add(eng, lambda e: e.tensor_copy(out, in_), [in_], [out])

    def memset(self, eng, out, val):
        return self.add(eng, lambda e: e.memset(out, val), [], [out])

    def emit(self, final_wait_ops=()):
        nc = self.nc
        ops = self.ops
        used = set()
        for o in ops:
            best = {}
            need = []
            for d in o["deps"]:
                dop = ops[d]
                if dop["dma"]:
                    need.append(d)
                    continue
                if dop["eng"] == "pe" and o["eng"] == "pe" and not o["dma"]:
                    continue
                if best.get(dop["eng"], -1) < d:
                    best[dop["eng"]] = d
            need += list(best.values())
            o["need"] = need
            used.update(need)
        for d in final_wait_ops:
            used.add(d)
        engs = list(ENG_ATTR.keys())
        import contextlib
        with contextlib.ExitStack() as st:
            esem = {e: st.enter_context(nc.semaphore("s_" + e)) for e in engs}
            dsems = [st.enter_context(nc.semaphore("d%d" % i)) for i in range(self.n_dma_sems)]
            cnt = {e: 0 for e in engs}
            dval = [0] * self.n_dma_sems
            dnext = 0
            for o in ops:
                if o["dma"]:
                    k = dnext
                    dnext = (dnext + 1) % self.n_dma_sems
                    o["prev_tok"] = ("d", k, dval[k])
                    dval[k] += 16
                    o["tok"] = ("d", k, dval[k])
                elif o["id"] in used:
                    cnt[o["eng"]] += 1
                    o["tok"] = ("e", o["eng"], cnt[o["eng"]])
                else:
                    o["tok"] = None
            seen = {e: {} for e in engs}

            def semof(tok):
                return esem[tok[1]] if tok[0] == "e" else dsems[tok[1]]

            plans = {e: [] for e in engs}
            for o in ops:
                e = o["eng"]
                sn = seen[e]
                waits = []
                need = []
                for d in o["need"]:
                    need.append(ops[d])
                if o["dma"] and o["prev_tok"][2] > 0:
                    need.append(dict(tok=o["prev_tok"], clock={}))
                for dop in need:
                    tok = dop["tok"]
                    key = tok[:2]
                    if sn.get(key, 0) >= tok[2]:
                        continue
                    waits.append(tok)
                    sn[key] = tok[2]
                    for k2, v2 in dop.get("clock", {}).items():
                        if sn.get(k2, 0) < v2:
                            sn[k2] = v2
                wm = {}
                for tok in waits:
                    wm[tok[:2]] = max(wm.get(tok[:2], 0), tok[2])
                if o["tok"] is not None:
                    o["clock"] = dict(sn)
                    if o["tok"][0] == "e":
                        o["clock"][o["tok"][:2]] = o["tok"][2]
                plans[e].append((o, wm))
            self.n_waits = sum(len(w) for e in engs for _, w in plans[e])
            final = [ops[d]["tok"] for d in final_wait_ops]
            with nc.Block() as block:
                def mk(e):
                    def body(engine):
                        for o, wm in plans[e]:
                            for key, v in wm.items():
                                engine.wait_ge(semof(key + (0,)), v)
                            inst = o["fn"](engine)
                            if o["tok"] is not None:
                                tok = o["tok"]
                                inst.then_inc(semof(tok), 16 if tok[0] == "d" else 1)
                        if e == "sync":
                            for tok in final:
                                engine.wait_ge(semof(tok), tok[2])
                    return body
                for e in engs:
                    getattr(block, ENG_ATTR[e])(mk(e))


T = 2048
TH = 1024
D = 1024
KC = 8
DFF = 2816
EPS = 1e-6
NPV = 512
PV_NM, PV_NF, PV_NFIN, PV_GCW, PV_GOG, PV_LCW, PV_LCB, PV_LBA, PV_LBX, PV_LAM, PV_FCW, PV_FCB = (
    0, 16, 32, 40, 88, 89, 121, 129, 137, 145, 153, 417)
CM_ID, CM_U, CM_SL, CM_PERM, CM_DMT, CM_QDC, CM_KTD, CM_CDC = 0, 128, 256, 384, 512, 1024, 1536, 1540
NCM = 1544
DK_SCALE = 128.0 ** -0.5


def host_consts():
    f8 = np.float64
    idx = np.arange(128)
    cm = np.zeros((128, NCM), np.float64)
    cm[:, CM_ID:CM_ID + 128] = np.eye(128)
    cm[:, CM_U:CM_U + 128] = (idx[:, None] <= idx[None, :])
    cm[:, CM_SL:CM_SL + 128] = (idx[:, None] > idx[None, :])
    perm = np.zeros((128, 128))
    perm[idx, (idx + 64) % 128] = 1.0
    cm[:, CM_PERM:CM_PERM + 128] = perm
    lg = np.log1p(-np.exp2(-5.0 - np.arange(4, dtype=f8)))
    for h in range(4):
        rel = idx[None, :] - idx[:, None]
        m = np.where(rel >= 0, np.exp(lg[h] * np.maximum(rel, 0)), 0.0) * DK_SCALE
        cm[:, CM_DMT + h * 128:CM_DMT + (h + 1) * 128] = m
        cm[:, CM_QDC + h * 128:CM_QDC + (h + 1) * 128] = np.exp(lg[h] * (idx + 1.0))[None, :]
        cm[:, CM_KTD + h] = np.exp(lg[h] * (127 - idx)) * DK_SCALE
        cm[:, CM_CDC + h] = np.exp(lg[h] * 128.0)
    half = 64
    inv_freq = (np.float32(10000.0) ** (-np.arange(half, dtype=np.float32) / np.float32(half))).astype(np.float32)
    ang = (np.arange(T, dtype=np.float32)[:, None] * inv_freq[None, :]).astype(np.float32)
    cos = np.cos(ang.astype(f8)).T
    sin = np.sin(ang.astype(f8)).T
    rope = np.zeros((128, 2 * T), np.float64)
    rope[0:64, 0:T] = cos
    rope[64:128, 0:T] = cos
    rope[0:64, T:2 * T] = -sin
    rope[64:128, T:2 * T] = sin
    return cm.astype(np.float32), rope.astype(np.float32)


def host_pvec(inp):
    pv = np.zeros((128, NPV), np.float32)

    def put(col, vec):
        n = vec.shape[0] // 128
        pv[:, col:col + n] = vec.reshape(n, 128).T

    for l in range(2):
        put(PV_NM + l * 8, inp["norm_mix"][l])
        put(PV_NF + l * 8, inp["norm_ffn"][l])
    put(PV_NFIN, inp["norm_final"])
    for j in range(4):
        put(PV_GCW + j * 12, inp["gdn_conv_w"][0, j])
        put(PV_LCW + j * 8, inp["lru_conv_w"][0, j])
    put(PV_GOG, inp["gdn_out_gain"][0])
    put(PV_LCB, inp["lru_conv_b"][0])
    put(PV_LBA, inp["lru_b_a"][0])
    put(PV_LBX, inp["lru_b_x"][0])
    put(PV_LAM, inp["lru_lambda"][0])
    for l in range(2):
        for j in range(3):
            put(PV_FCW + l * 132 + j * 44, inp["ffn_conv_w"][l, j])
        put(PV_FCB + l * 44, inp["ffn_conv_b"][l])
    hv = np.zeros((128, 8), np.float32)
    hv[:, 0:4] = inp["gdn_dt_bias"][0][None, :]
    hv[:, 4:8] = inp["gdn_a_log"][0][None, :]
    return pv, hv


def build(stage=99, dump=None):
    nc = bass.Bass("TRN2", target_bir_lowering=False)
    P = Prog(nc)
    IN = "ExternalInput"
    x_d = P.dram("x", [T, D], F32, IN)
    w_in0 = P.dram("w_in0", [D, 4104], F32, IN)
    w_out0 = P.dram("w_out0", [D, D], F32, IN)
    lw_in = P.dram("lw_in", [D, 2048], F32, IN)
    lw_a = P.dram("lw_a", [8, 128, 128], F32, IN)
    lw_x = P.dram("lw_x", [8, 128, 128], F32, IN)
    lw_out = P.dram("lw_out", [D, D], F32, IN)
    f_up = P.dram("f_up", [2, D, 2 * DFF], F32, IN)
    f_dn = P.dram("f_dn", [2, DFF, D], F32, IN)
    pvec_d = P.dram("pvec", [128, NPV], F32, IN)
    hvec_d = P.dram("hvec", [128, 8], F32, IN)
    cmat_d = P.dram("cmat", [128, NCM], F32, IN)
    rope_d = P.dram("rope", [128, 2 * T], F32, IN)
    out_d = P.dram("out", [T, D], F32, "ExternalOutput")
    dump_d = {}

    banks = [P.ps("bk%d" % i, [128, 512], F32) for i in range(8)]
    st = dict(b=0, q=0)

    held = set()

    def nb(hold=False):
        for _ in range(8):
            i = st["b"] % 8
            st["b"] += 1
            if i not in held:
                if hold:
                    held.add(i)
                return banks[i]
        raise AssertionError("all PSUM banks held")

    def rel(b):
        held.discard(banks.index(b))

    hT = P.sb("hT", [128, KC, T], F32)
    pv = P.sb("pv", [128, NPV], F32)
    hv = P.sb("hv", [128, 8], F32)
    cm = P.sb("cm", [128, NCM], F32)
    ident = cm[:, CM_ID:CM_ID + 128]
    cb = P.sb("cb", [128, 512], BF16)
    ident_bf, U_bf, SL_bf, perm_bf = (cb[:, i * 128:(i + 1) * 128] for i in range(4))
    ones_bf = P.sb("ones_bf", [128, 128], BF16)
    MiT = P.sb("MiT", [128, 128], F32)
    dmaskT = cm[:, CM_DMT:CM_DMT + 512].rearrange("p (h i) -> p h i", h=4)
    qdec_c = cm[:, CM_QDC:CM_QDC + 512].rearrange("p (h i) -> p h i", h=4)
    Ms = cm[:, CM_SL:CM_SL + 128]
    hn = P.sb("hn", [128, KC, TH], BF16)
    yb = P.sb("yb", [128, KC, TH], BF16)
    Sr32 = P.sb("Sr32", [128, 4, 128], F32)
    Srb = P.sb("Srb", [128, 4, 128], BF16)
    Sg32 = P.sb("Sg32", [128, 4, 128], F32)
    Sgb = P.sb("Sgb", [128, 4, 128], BF16)
    ghalo = P.sb("ghalo", [128, 12, 3], F32)
    fhalo = P.sb("fhalo", [128, 44, 2], F32)
    lhalo = P.sb("lhalo", [128, 8, 3], F32)
    lstate = P.sb("lstate", [128, 8], F32)
    lsc = P.sb("lsc", [128, 16], F32)
    negA = P.sb("negA", [128, 4], F32)
    nlb = P.sb("nlb", [128, 16], F32)
    SCR = P.sb_top

    P.dma(pv[:], pvec_d.ap())
    P.dma(hv[:], hvec_d.ap())
    P.dma(cm[:], cmat_d.ap())
    P.copy("dve", cb[:], cm[:, 0:512])
    P.memset("pool", ones_bf[:], 1.0)
    P.ts("dve", MiT[:], cm[:, CM_U:CM_U + 128], DK_SCALE, ALU.mult)
    P.act(negA[:], hv[:, 4:8], AF.Exp)
    P.ts("dve", negA[:], negA[:], -1.0, ALU.mult)
    P.act(lsc[:, 0:8], pv[:, PV_LAM:PV_LAM + 8], AF.Exp, scale=-1.0)
    P.act(lsc[:, 0:8], lsc[:, 0:8], AF.Ln, bias=1.0)
    P.ts("dve", lsc[:, 8:16], lsc[:, 0:8], -16.0, ALU.mult)
    P.ts("dve", lsc[:, 0:8], lsc[:, 0:8], -8.0, ALU.mult)
    P.ts("dve", nlb[:, 0:8], pv[:, PV_LBA:PV_LBA + 8], -1.0, ALU.mult)
    P.ts("dve", nlb[:, 8:16], pv[:, PV_LBX:PV_LBX + 8], -1.0, ALU.mult)
    for t_ in (Sr32, Srb, Sg32, Sgb, lstate):
        P.memset("pool", t_[:], 0.0)

    P.phases = []

    def phase(name):
        P.phases.append((name, sum(1 for o in P.ops if o["eng"] == "pe")))

    def add_dump(name, ap, shape):
        if dump is None or name not in dump:
            return
        d = P.dram("dump_" + name, list(shape), ap.dtype, "ExternalOutput")
        dump_d[name] = P.dma(d.ap(), ap)

    P.sb_top = SCR
    xt = [P.sb("xt%d" % i, [128, D], F32) for i in range(2)]
    def load_x(tc, bufs=None):
        xb = (bufs or xt)[tc % 2]
        P.dma(xb[:], x_d.ap()[tc * 128:(tc + 1) * 128, :])
        for g in range(2):
            bk = nb()
            for j in range(4):
                c = g * 4 + j
                P.tr(bk[:, j * 128:(j + 1) * 128], xb[:, c * 128:(c + 1) * 128], ident)
            P.copy("dve" if g == 0 else "act", hT[:, g * 4:(g + 1) * 4, tc * 128:(tc + 1) * 128],
                   bk[:].rearrange("p (j t) -> p j t", j=4))
    for tc in range(8):
        load_x(tc)

    def rmsnorm_half(hf, gcol):
        mark = P.sb_top
        sq = P.sb("rn_sq", [128, KC, 512], BF16)
        rstd = P.sb("rn_rstd", [128, 512], F32)
        for tb in range(2):
            ts_ = slice(hf * TH + tb * 512, hf * TH + (tb + 1) * 512)
            for c in range(KC):
                if c % 2 == 0:
                    P.act(sq[:, c, :], hT[:, c, ts_], AF.Square)
                else:
                    P.tt("pool", sq[:, c, :], hT[:, c, ts_], hT[:, c, ts_], ALU.mult)
            bk = nb()
            for c in range(KC):
                P.mm(bk[:], ones_bf[:], sq[:, c, :], start=(c == 0), stop=(c == KC - 1))
            P.act(rstd[:], bk[:], AF.Ln, scale=1.0 / D, bias=EPS)
            P.act(rstd[:], rstd[:], AF.Exp, scale=-0.5)
            for c in range(KC):
                P.stt("dve", hn[:, c, tb * 512:(tb + 1) * 512], hT[:, c, ts_], pv[:, gcol + c:gcol + c + 1],
                      rstd[:], ALU.mult, ALU.mult)
        P.sb_top = mark

    wcache = {}

    class WBuf:
        def __init__(self, tag, cast="act", size=2816):
            self.st = [P.sb("wst%s%d" % (tag, i), [128, size], F32) for i in range(2)]
            self.bf = [P.sb("wbf%s%d" % (tag, i), [128, size], BF16) for i in range(2)]
            self.i = 0
            self.cast = cast
            self.pend = None

        def get(self, spec, kcn, hf):
            if self.pend is not None and self.pend[0] == spec[1]:
                bfv = self.pend[1]
                self.pend = None
                return bfv
            return self.load(spec[0], kcn, spec[1], hf)

        def prefetch(self, spec, kcn, hf):
            self.pend = (spec[1], self.load(spec[0], kcn, spec[1], hf))

        def load(self, srcs, kcn, key, hf):
            ntot = sum(s_.shape[2] for s_ in srcs)
            tot = kcn * ntot
            i = self.i % 2
            self.i += 1
            bfv = self.bf[i][:, 0:tot].rearrange("p (k n) -> p k n", k=kcn)
            if hf == 1 and key in wcache:
                P.dma(self.bf[i][:, 0:tot], wcache[key].ap())
                return bfv
            stv = self.st[i][:, 0:tot].rearrange("p (k n) -> p k n", k=kcn)
            off = 0
            for s_ in srcs:
                n = s_.shape[2]
                P.dma(stv[:, :, off:off + n], s_)
                off += n
            P.copy(self.cast, self.bf[i][:, 0:tot], self.st[i][:, 0:tot])
            if hf == 0:
                wcache[key] = P.dram("wc_" + key, [128, tot], BF16, "Internal")
                P.dma(wcache[key].ap(), self.bf[i][:, 0:tot], eng="act")
            return bfv

    def kview(ap2):
        return ap2.rearrange("(k p) n -> p k n", p=128)

    def run_pipe(active):
        for g in list(reversed(active)):
            try:
                next(g)
            except StopIteration:
                active.remove(g)

    def proj_fm(wb, specs, kcn, rhs_fn, consume, hf, after=None):
        active = []
        nxt = wb.get(specs[0], kcn, hf)
        for bi in range(len(specs)):
            bfv = nxt
            if bi + 1 < len(specs):
                nxt = wb.load(specs[bi + 1][0], kcn, specs[bi + 1][1], hf)
            elif after is not None:
                wb.prefetch(after[0], after[1], hf)
            nch = bfv.shape[2] // 128
            for tb in range(2):
                for ci in range(nch):
                    bk = nb()
                    for kc in range(kcn):
                        P.mm(bk[:], bfv[:, kc, ci * 128:(ci + 1) * 128], rhs_fn(kc, tb),
                             start=(kc == 0), stop=(kc == kcn - 1))
                    g = consume(bi, ci, tb, bk)
                    if g is not None:
                        active.append(g)
                    run_pipe(active)
        while active:
            run_pipe(active)

    def conv_block(bk, stage, W, wcol, bias, acc, halo_next):
        P.copy("act", stage[:, W - 1:W - 1 + 512], bk[:])
        if bias is None:
            P.act(acc, bk[:], AF.Copy, scale=wcol(W - 1))
        else:
            P.act(acc, bk[:], AF.Identity, scale=wcol(W - 1), bias=bias)
        for j in range(W - 1):
            P.stt("dve", acc, stage[:, j:j + 512], wcol(j), acc, ALU.mult, ALU.add)
        if halo_next is not None:
            P.copy("pool", halo_next, stage[:, 512:512 + W - 1])

    def v4(bank):
        return bank[:].rearrange("p (h i) -> p h i", h=4)

    def v4b(bank):
        return bank[:].bitcast(BF16)[:, 0:512].rearrange("p (h i) -> p h i", h=4)

    def bc4(ap2):
        return ap2.unsqueeze(2).to_broadcast([128, 4, 128])

    def out_norm(bo, gain, gate, yout, tmp):
        sq, r, t = tmp
        P.act(sq[:], bo[:], AF.Square)
        bs = nb()
        P.mm(bs[:], ones_bf[:], sq[:])
        P.act(r[:], bs[:], AF.Ln, scale=1.0 / 128, bias=EPS)
        P.act(r[:], r[:], AF.Exp, scale=-0.5)
        if gain is None:
            P.tt("dve", t[:], bo[:], r[:], ALU.mult)
        else:
            P.stt("dve", t[:], bo[:], gain, r[:], ALU.mult, ALU.mult)
        P.tt("pool", yout, t[:].rearrange("p (h i) -> p h i", h=4), gate, ALU.mult)

    def out_norm_g(bo, gain, gate, yout, tmp):
        sq, r, t = tmp
        P.act(sq, bo[:], AF.Square)
        yield
        bs = nb()
        P.mm(bs[:], ones_bf[:], sq)
        P.act(r, bs[:], AF.Ln, scale=1.0 / 128, bias=EPS)
        P.act(r, r, AF.Exp, scale=-0.5)
        yield
        if gain is None:
            P.tt("dve", t, bo[:], r, ALU.mult)
        else:
            P.stt("dve", t, bo[:], gain, r, ALU.mult, ALU.mult)
        rel(bo)
        yield
        P.tt("pool", yout, t.rearrange("p (h i) -> p h i", h=4), gate, ALU.mult)

    def out_proj(wb, wd, hf, key):
        def consume(bi, ci, tb, bk):
            dc = bi * 2 + ci
            ts_ = slice(hf * TH + tb * 512, hf * TH + (tb + 1) * 512)
            P.tt("dve", hT[:, dc, ts_], hT[:, dc, ts_], bk[:], ALU.add)
        proj_fm(wb, [([kview(wd[:, b * 256:(b + 1) * 256])], "%s_%d" % (key, b)) for b in range(4)], 8,
                lambda kc, tb: yb[:, kc, tb * 512:(tb + 1) * 512], consume, hf)

    def mixer0(hf):
        w = w_in0.ap()
        phase('m0_norm%d' % hf)
        rmsnorm_half(hf, PV_NM)
        phase('m0_retproj%d' % hf)
        m0 = P.sb_top
        qr = P.sb("qr", [128, 4, TH], BF16)
        kr = P.sb("kr", [128, 4, TH], BF16)
        vtok = P.sb("vtok", [128, 8, 512], BF16)
        gr = P.sb("gr", [128, 4, TH], BF16)
        m1 = P.sb_top
        wb = WBuf("a", cast="dve", size=2048)
        ropec = P.sb("ropec", [128, TH], F32)
        ropes = P.sb("ropes", [128, TH], F32)
        P.dma(ropec[:], rope_d.ap()[:, hf * TH:(hf + 1) * TH])
        P.dma(ropes[:], rope_d.ap()[:, T + hf * TH:T + (hf + 1) * TH])
        NBUF = 4
        xbf = [P.sb("xbf%d" % i, [128, 512], BF16) for i in range(NBUF)]
        t1 = [P.sb("t1_%d" % i, [128, 512], F32) for i in range(NBUF)]
        t2 = [P.sb("t2_%d" % i, [128, 512], F32) for i in range(NBUF)]
        cnt = [0]
        hn_rhs = lambda kc, tb: hn[:, kc, tb * 512:(tb + 1) * 512]

        def wspec(c0, n=256):
            return ([kview(w[:, c0:c0 + n])], "win0_%d" % c0)

        def cons_qk(bi, ci, tb, bk):
            cc = bi * 2 + ci
            dst = qr if cc < 4 else kr
            i = cnt[0] % NBUF
            cnt[0] += 1
            tsl = slice(tb * 512, (tb + 1) * 512)
            P.copy("act", xbf[i][:], bk[:])
            P.tt("dve", t1[i][:], bk[:], ropec[:, tsl], ALU.mult)
            yield
            b2 = nb(hold=True)
            P.mm(b2[:], perm_bf, xbf[i][:])
            yield
            P.tt("dve", t2[i][:], b2[:], ropes[:, tsl], ALU.mult)
            rel(b2)
            P.tt("pool", dst[:, cc % 4, tsl], t1[i][:], t2[i][:], ALU.add)
        proj_fm(wb, [wspec(b * 256) for b in range(4)], 8, hn_rhs, cons_qk, hf, after=(wspec(1024), 8))
        nxt = wb.get(wspec(1024), 8, hf)
        for blk in range(2):
            bfv = nxt
            if blk == 0:
                nxt = wb.load(wspec(1280)[0], 8, wspec(1280)[1], hf)
            else:
                wb.prefetch(wspec(1536), 8, hf)
            for tc in range(8):
                bk = nb()
                for kc in range(8):
                    P.mm(bk[:, 0:256], hn[:, kc, tc * 128:(tc + 1) * 128], bfv[:, kc, :],
                         start=(kc == 0), stop=(kc == 7))
                P.copy("act", vtok[:, tc, blk * 256:(blk + 1) * 256], bk[:, 0:256])

        def cons_g(dst):
            def f(bi, ci, tb, bk):
                P.act(dst[:, bi * 2 + ci, tb * 512:(tb + 1) * 512], bk[:], AF.Silu)
            return f
        proj_fm(wb, [wspec(1536 + b * 256) for b in range(2)], 8, hn_rhs, cons_g(gr), hf)
        add_dump("qr%d" % hf, qr[:], [128, 4, TH])
        add_dump("kr%d" % hf, kr[:], [128, 4, TH])
        add_dump("vtok%d" % hf, vtok[:], [128, 8, 512])
        phase('m0_retrec%d' % hf)
        P.sb_top = m1
        ktd_bc = bc4(cm[:, CM_KTD:CM_KTD + 4])
        cdc_bc = bc4(cm[:, CM_CDC:CM_CDC + 4])
        rslots = []
        for sl in range(3):
            rslots.append(dict(
                ktb=P.sb("r%d_kt" % sl, [128, 4, 128], BF16), smb=P.sb("r%d_sm" % sl, [128, 4, 128], BF16),
                qdb=P.sb("r%d_qd" % sl, [128, 4, 128], BF16), stmp=P.sb("r%d_stmp" % sl, [128, 4, 128], F32),
                ntmp=(P.sb("n%d_sq" % sl, [128, 512], BF16)[:], P.sb("n%d_r" % sl, [128, 512], F32)[:],
                      P.sb("n%d_t" % sl, [128, 512], F32)[:])))

        def ret_chunk(n, B):
            cs = slice(n * 128, (n + 1) * 128)
            ktb, smb, qdb, stmp = B["ktb"], B["smb"], B["qdb"], B["stmp"]
            b1 = nb()
            for h in range(4):
                P.tr(v4b(b1)[:, h, :], kr[:, h, cs], ident_bf)
            b2 = nb()
            for h in range(4):
                P.mm(v4(b2)[:, h, :], kr[:, h, cs], qr[:, h, cs])
            P.tt("pool", qdb[:], qr[:, :, cs], qdec_c, ALU.mult)
            P.tt("dve", ktb[:], v4b(b1), ktd_bc, ALU.mult)
            P.tt("dve", smb[:], v4(b2), dmaskT, ALU.mult)
            yield
            b3 = nb(hold=True)
            for h in range(4):
                P.mm(v4(b3)[:, h, :], ktb[:, h, :], vtok[:, n, h * 128:(h + 1) * 128])
            b4 = nb(hold=True)
            for h in range(4):
                P.mm(v4(b4)[:, h, :], vtok[:, n, h * 128:(h + 1) * 128], smb[:, h, :], start=True, stop=False)
                P.mm(v4(b4)[:, h, :], Srb[:, h, :], qdb[:, h, :], start=False, stop=True)
            P.tt("pool", stmp[:], Sr32[:], cdc_bc, ALU.mult)
            yield
            P.tt("dve", Sr32[:], stmp[:], v4(b3), ALU.add)
            rel(b3)
            P.copy("act", Srb[:], Sr32[:])
            yield from out_norm_g(b4, None, gr[:, :, cs], yb[:, 0:4, cs], B["ntmp"])

        xq = list(range(8, 16)) if hf == 0 else []
        xt2 = [P.sb("xt2_%d" % i, [128, D], F32) for i in range(2)] if hf == 0 else None
        active = []
        nstart = 0
        step = 0
        while nstart < 8 or active:
            if nstart < 8 and len(active) < 3 and step >= nstart * 2:
                active.append(ret_chunk(nstart, rslots[nstart % 3]))
                nstart += 1
            run_pipe(active)
            step += 1
            if xq and step % 2 == 0:
                load_x(xq.pop(0), xt2)
        while xq:
            load_x(xq.pop(0), xt2)
        add_dump("yr%d" % hf, yb[:, 0:4, :], [128, 4, TH])
        if stage < 2:
            P.sb_top = m0
            return
        phase('m0_gdnproj%d' % hf)
        P.sb_top = m0
        qd = P.sb("qd", [128, 4, TH], BF16)
        kd = P.sb("kd", [128, 4, TH], BF16)
        vd = P.sb("vd", [128, 4, TH], BF16)
        gd = yb[:, 4:8, :]
        tm_pre = P.sb("tm_pre", [128, 8, 8], F32)
        m1 = P.sb_top
        wb = WBuf("b", cast="dve", size=2048)
        stg = [[P.sb("g_st%d_%d" % (ci, i), [128, 515], F32) for i in range(2)] for ci in range(2)]
        NBUF = 7
        acc = [P.sb("g_acc%d" % i, [128, 512], F32) for i in range(NBUF)]
        sqv_ = [P.sb("g_sq%d" % i, [128, 512], BF16) for i in range(3)]
        rv_ = [P.sb("g_r%d" % i, [128, 512], F32) for i in range(5)]
        cnt = [0]

        def cons_conv(bi, ci, tb, bk):
            cc = bi * 2 + ci
            i = cnt[0] % NBUF
            sqv = {i: sqv_[cnt[0] % 3]}
            rv = {i: rv_[cnt[0] % 5]}
            cnt[0] += 1
            tsl = slice(tb * 512, (tb + 1) * 512)
            sg = stg[ci][tb % 2]
            W = 4
            wcol = lambda j: pv[:, PV_GCW + j * 12 + cc:PV_GCW + j * 12 + cc + 1]
            if tb == 0:
                if hf == 0:
                    P.memset("pool", sg[:, 0:3], 0.0)
                else:
                    P.copy("pool", sg[:, 0:3], ghalo[:, cc, :])
                hnext = stg[ci][1][:, 0:3]
            else:
                hnext = ghalo[:, cc, :] if hf == 0 else None
            P.copy("act", sg[:, W - 1:W - 1 + 512], bk[:])
            P.act(acc[i][:], bk[:], AF.Copy, scale=wcol(W - 1))
            yield
            for j in range(W - 1):
                P.stt("dve", acc[i][:], sg[:, j:j + 512], wcol(j), acc[i][:], ALU.mult, ALU.add)
            if hnext is not None:
                P.copy("pool", hnext, sg[:, 512:512 + W - 1])
            yield
            P.act(rv[i][:], acc[i][:], AF.Exp, scale=-1.0)
            P.act(rv[i][:], rv[i][:], AF.Ln, bias=1.0)
            P.act(rv[i][:], rv[i][:], AF.Exp, scale=-1.0)
            yield
            if cc >= 8:
                P.tt("dve", vd[:, cc - 8, tsl], acc[i][:], rv[i][:], ALU.mult)
                return
            dst = qd if cc < 4 else kd
            P.tt("dve", acc[i][:], acc[i][:], rv[i][:], ALU.mult)
            P.tt("pool", sqv[i][:], acc[i][:], acc[i][:], ALU.mult)
            yield
            b2 = nb(hold=True)
            P.mm(b2[:], ones_bf[:], sqv[i][:])
            yield
            P.act(rv[i][:], b2[:], AF.Ln, bias=EPS)
            rel(b2)
            P.act(rv[i][:], rv[i][:], AF.Exp, scale=-0.5)
            yield
            P.tt("dve", dst[:, cc % 4, tsl], acc[i][:], rv[i][:], ALU.mult)
        proj_fm(wb, [wspec(2048 + b * 256) for b in range(6)], 8, hn_rhs, cons_conv, hf, after=(wspec(3584), 8))
        proj_fm(wb, [wspec(3584 + b * 256) for b in range(2)], 8, hn_rhs, cons_g(gd), hf, after=(wspec(4096, 8), 8))
        bfv = wb.get(wspec(4096, 8), 8, hf)
        bk = nb()
        for tc in range(8):
            for kc in range(8):
                P.mm(bk[:, tc * 8:(tc + 1) * 8], hn[:, kc, tc * 128:(tc + 1) * 128], bfv[:, kc, :],
                     start=(kc == 0), stop=(kc == 7))
        P.copy("dve", tm_pre[:], bk[:, 0:64].rearrange("p (n c) -> p n c", c=8))
        add_dump("qd%d" % hf, qd[:], [128, 4, TH])
        add_dump("kd%d" % hf, kd[:], [128, 4, TH])
        add_dump("vd%d" % hf, vd[:], [128, 4, TH])
        add_dump("tmpre%d" % hf, tm_pre[:], [128, 8, 8])
        P.sb_top = m1
        phase('m0_gdnrec%d' % hf)
        def tm(name, dt=F32):
            return P.sb("tm_" + name, [128, 8, 4], dt)
        beta, nbeta, gtk, gc, gmg, egc, bge, ekt, tmp1 = (tm(n_) for n_ in
            ("beta", "nbeta", "g", "gc", "gmg", "egc", "bge", "ekt", "tmp1"))
        ghi, glo, gchi, gclo = (tm(n_, BF16) for n_ in ("ghi", "glo", "gchi", "gclo"))
        P.act(beta[:], tm_pre[:, :, 0:4], AF.Sigmoid)
        P.ts("dve", nbeta[:], beta[:], -1.0, ALU.mult)
        P.tt("dve", tmp1[:], tm_pre[:, :, 4:8], hv[:, 0:4].unsqueeze(1).to_broadcast([128, 8, 4]), ALU.add)
        P.act(tmp1[:], tmp1[:], AF.Exp)
        P.act(tmp1[:], tmp1[:], AF.Ln, bias=1.0)
        P.tt("dve", gtk[:], tmp1[:], negA[:].unsqueeze(1).to_broadcast([128, 8, 4]), ALU.mult)
        P.copy("dve", ghi[:], gtk[:])
        P.tt("dve", glo[:], gtk[:], ghi[:], ALU.subtract)
        bk = nb()
        for n in range(8):
            P.mm(bk[:, n * 4:(n + 1) * 4], U_bf, ghi[:, n, :], start=True, stop=False)
            P.mm(bk[:, n * 4:(n + 1) * 4], U_bf, glo[:, n, :], start=False, stop=True)
        for n in range(8):
            P.mm(bk[:, 64 + n * 4:64 + (n + 1) * 4], SL_bf, ghi[:, n, :], start=True, stop=False)
            P.mm(bk[:, 64 + n * 4:64 + (n + 1) * 4], SL_bf, glo[:, n, :], start=False, stop=True)
        P.copy("dve", gc[:], bk[:, 0:32].rearrange("p (n c) -> p n c", c=4))
        P.copy("dve", gmg[:], bk[:, 64:96].rearrange("p (n c) -> p n c", c=4))
        P.act(egc[:], gc[:], AF.Exp)
        P.tt("dve", bge[:], beta[:], egc[:], ALU.mult)
        P.act(ekt[:], gmg[:], AF.Exp)
        P.copy("dve", gchi[:], gc[:])
        P.tt("dve", gclo[:], gc[:], gchi[:], ALU.subtract)
        add_dump("gc%d" % hf, gc[:], [128, 8, 4])
        add_dump("beta%d" % hf, beta[:], [128, 8, 4])
        LNS = float(np.log(DK_SCALE))
        ident_bc = ident_bf.unsqueeze(1).to_broadcast([128, 4, 128])
        MiT_bc = MiT[:].unsqueeze(1).to_broadcast([128, 4, 128])
        Ms_bc = Ms.unsqueeze(1).to_broadcast([128, 4, 128])
        slots = []
        NSLOT = 3
        for sl in range(NSLOT):
            def f4(name, dt=BF16, sl=sl):
                return P.sb("g%d_%s" % (sl, name), [128, 4, 128], dt)
            d = dict(sl=sl)
            for n_ in ("eT", "eE", "EGb", "u32"):
                d[n_] = f4(n_, F32)
            d["stmp"] = slots[0]["stmp"] if slots else f4("stmp", F32)
            for n_ in ("N", "M", "N2", "M2", "N3", "M3", "XT", "Rk", "Rv", "ktl", "attT", "qdT",
                       "wT", "vnb"):
                d[n_] = f4(n_)
            d["dhi"], d["dlo"] = d["N2"], d["M2"]
            d["cdc"] = P.sb("g%d_cd" % sl, [128, 4], F32)
            slots.append(d)

        def gdn_chunk(n, B):
            cs = slice(n * 128, (n + 1) * 128)
            eT, eE, EGb, u32, stmp, dhi, dlo = (B[k] for k in ("eT", "eE", "EGb", "u32", "stmp", "dhi", "dlo"))
            Nb, Mb, XT, Rk, Rv, ktl, attT, qdT, wT, vnb, cdc = (B[k] for k in
                ("N", "M", "XT", "Rk", "Rv", "ktl", "attT", "qdT", "wT", "vnb", "cdc"))
            P.tt("pool", dhi[:], ident_bc, bc4(gchi[:, n, :]), ALU.mult)
            P.tt("pool", dlo[:], ident_bc, bc4(gclo[:, n, :]), ALU.mult)
            yield
            gb = nb()
            P.mm(gb[:], ones_bf[:], dhi[:].rearrange("p h i -> p (h i)"), start=True, stop=False)
            P.mm(gb[:], ones_bf[:], dlo[:].rearrange("p h i -> p (h i)"), start=False, stop=True)
            gbv = v4(gb)
            gcb = bc4(gc[:, n, :])
            P.tt("dve", eT[:], gbv, gcb, ALU.min)
            P.tt("dve", eE[:], gbv, gcb, ALU.max)
            P.act(EGb[:], gbv, AF.Exp, bias=LNS)
            P.act(cdc[:].unsqueeze(2), gbv[:, :, 127:128], AF.Exp)
            b4 = nb()
            for h in range(4):
                P.tr(v4b(b4)[:, h, :], kd[:, h, cs], ident_bf)
            P.tt("dve", Rk[:], v4b(b4), bc4(bge[:, n, :]), ALU.mult)
            P.tt("dve", ktl[:], v4b(b4), bc4(ekt[:, n, :]), ALU.mult)
            yield
            P.tt("pool", eT[:], eT[:], gcb, ALU.subtract)
            P.tt("pool", eE[:], eE[:], gcb, ALU.subtract)
            b5 = nb()
            for h in range(4):
                P.tr(v4b(b5)[:, h, :], vd[:, h, cs], ident_bf)
            P.tt("dve", Rv[:], v4b(b5), bc4(beta[:, n, :]), ALU.mult)
            yield
            P.act(eT[:], eT[:], AF.Exp)
            P.act(eE[:], eE[:], AF.Exp, scale=-1.0)
            P.tt("pool", qdT[:], qd[:, :, cs], EGb[:], ALU.mult)
            yield
            P.tt("pool", eT[:], eT[:], MiT_bc, ALU.mult)
            P.tt("pool", eE[:], eE[:], Ms_bc, ALU.mult)
            P.tt("pool", eE[:], eE[:], bc4(nbeta[:, n, :]), ALU.mult)
            yield
            b1 = nb()
            for h in range(4):
                P.mm(v4(b1)[:, h, :], kd[:, h, cs], kd[:, h, cs])
            b2 = nb()
            for h in range(4):
                P.mm(v4(b2)[:, h, :], kd[:, h, cs], qd[:, h, cs])
            P.tt("dve", Nb[:], v4(b1), eE[:], ALU.mult)
            P.tt("dve", attT[:], v4(b2), eT[:], ALU.mult)
            yield
            b3 = nb()
            for h in range(4):
                P.tr(v4b(b3)[:, h, :], Nb[:, h, :], ident_bf)
            P.copy("act", Mb[:], v4b(b3))
            yield
            P.tt("pool", XT[:], Mb[:], ident_bc, ALU.add)
            Nc, Mc = Nb, Mb
            pairs = [(B["N2"], B["M2"]), (B["N3"], B["M3"])]
            for k in range(6):
                Nn, Mn = pairs[k % 2]
                ba = nb()
                for h in range(4):
                    P.mm(v4(ba)[:, h, :], Mc[:, h, :], Nc[:, h, :])
                if k < 5:
                    bb = nb()
                    for h in range(4):
                        P.mm(v4(bb)[:, h, :], Nc[:, h, :], Mc[:, h, :])
                P.copy("act", Nn[:], v4(ba))
                if k < 5:
                    P.copy("dve", Mn[:], v4(bb))
                yield
                bx = nb()
                for h in range(4):
                    P.mm(v4(bx)[:, h, :], Nn[:, h, :], XT[:, h, :])
                P.tt("dve", XT[:], XT[:], v4(bx), ALU.add)
                yield
                Nc, Mc = Nn, Mn
            b6 = nb()
            for h in range(4):
                P.mm(v4(b6)[:, h, :], XT[:, h, :], Rv[:, h, :])
            b7 = nb()
            for h in range(4):
                P.mm(v4(b7)[:, h, :], Rk[:, h, :], XT[:, h, :])
            P.copy("act", u32[:], v4(b6))
            P.copy("act", wT[:], v4(b7))
            yield
            b8 = nb()
            for h in range(4):
                P.mm(v4(b8)[:, h, :], wT[:, h, :], Sgb[:, h, :])
            P.tt("dve", vnb[:], u32[:], v4(b8), ALU.subtract)
            yield
            b9 = nb()
            for h in range(4):
                P.mm(v4(b9)[:, h, :], ktl[:, h, :], vnb[:, h, :])
            bo = nb(hold=True)
            for h in range(4):
                P.mm(v4(bo)[:, h, :], Sgb[:, h, :], qdT[:, h, :], start=True, stop=False)
                P.mm(v4(bo)[:, h, :], vnb[:, h, :], attT[:, h, :], start=False, stop=True)
            P.tt("pool", stmp[:], Sg32[:], bc4(cdc[:]), ALU.mult)
            P.tt("dve", Sg32[:], stmp[:], v4(b9), ALU.add)
            P.copy("act", Sgb[:], Sg32[:])
            yield
            ntmp = (dhi[:].rearrange("p h i -> p (h i)"), eE[:].rearrange("p h i -> p (h i)"),
                    eT[:].rearrange("p h i -> p (h i)"))
            yield from out_norm_g(bo, pv[:, PV_GOG:PV_GOG + 1], gd[:, :, cs], yb[:, 4:8, cs], ntmp)

        active = []
        NSTAGE = 27
        nstart = 0
        step = 0
        while nstart < 8 or active:
            if nstart < 8 and len(active) < NSLOT and step >= nstart * (NSTAGE // NSLOT):
                active.append(gdn_chunk(nstart, slots[nstart % NSLOT]))
                nstart += 1
            run_pipe(active)
            step += 1
        add_dump("yd%d" % hf, yb[:, 4:8, :], [128, 4, TH])
        P.sb_top = m0
        if stage < 3:
            return
        phase('m0_outproj%d' % hf)
        wb = WBuf("c", cast="dve", size=2048)
        out_proj(wb, w_out0.ap(), hf, "wout0")
        P.sb_top = m0

    def ffn(l, hf, inject=None):
        phase('ffn%d_up%d' % (l, hf))
        rmsnorm_half(hf, PV_NF + l * 8)
        m0 = P.sb_top
        mb = P.sb("f_m", [128, 22, TH], BF16)
        wb = WBuf("f")
        stg = [[P.sb("f_st%d_%d" % (ci, i), [128, 514], F32) for i in range(2)] for ci in range(2)]
        NBUF = 3
        yb_at = P.base[yb.name][1]
        accg = [P.sb("f_ag%d" % i, [128, 512], F32, at=yb_at + i * 2048) for i in range(NBUF)]
        accv = [P.sb("f_av%d" % i, [128, 512], F32, at=yb_at + (NBUF + i) * 2048) for i in range(NBUF)]
        wup = f_up.ap()[l]
        wdn = f_dn.ap()[l]
        dspec = [([kview(wdn[:, dc * 128:(dc + 1) * 128])], "dn%d_%d" % (l, dc)) for dc in range(8)]
        cw, cbias = PV_FCW + l * 132, PV_FCB + l * 44
        cnt = [0]

        def cons(bi, ci, tb, bk):
            g = bi
            cc = g if ci == 0 else 22 + g
            i = cnt[0] % NBUF
            if ci == 1:
                cnt[0] += 1
            sg = stg[ci][tb % 2]
            if tb == 0:
                if hf == 0:
                    P.memset("pool", sg[:, 0:2], 0.0)
                else:
                    P.copy("pool", sg[:, 0:2], fhalo[:, cc, :])
                hnext = stg[ci][1][:, 0:2]
            else:
                hnext = fhalo[:, cc, :] if hf == 0 else None
            acc = accg[i] if ci == 0 else accv[i]
            conv_block(bk, sg, 3, lambda j: pv[:, cw + j * 44 + cc:cw + j * 44 + cc + 1],
                       pv[:, cbias + cc:cbias + cc + 1], acc[:], hnext)
            if ci == 0:
                return
            yield
            yield
            P.act(accg[i][:], accg[i][:], AF.Silu)
            P.tt("pool", mb[:, g, tb * 512:(tb + 1) * 512], accg[i][:], accv[i][:], ALU.mult)
        proj_fm(wb, [([kview(wup[:, g * 128:(g + 1) * 128]), kview(wup[:, DFF + g * 128:DFF + (g + 1) * 128])],
                      "up%d_%d" % (l, g)) for g in range(22)], 8,
                lambda kc, tb: hn[:, kc, tb * 512:(tb + 1) * 512], cons, hf, after=(dspec[0], 22))
        phase('ffn%d_down%d' % (l, hf))
        pipe = inject(store_slots(2, ot_at=P.base[stg[0][0].name][1])) if inject is not None else None
        nxt = wb.get(dspec[0], 22, hf)
        for dc in range(8):
            bfv = nxt
            if dc < 7:
                nxt = wb.load(dspec[dc + 1][0], 22, dspec[dc + 1][1], hf)
            for tb in range(2):
                bk = nb()
                for g in range(22):
                    P.mm(bk[:], bfv[:, g, :], mb[:, g, tb * 512:(tb + 1) * 512], start=(g == 0), stop=(g == 21))
                ts_ = slice(hf * TH + tb * 512, hf * TH + (tb + 1) * 512)
                P.tt("dve", hT[:, dc, ts_], hT[:, dc, ts_], bk[:], ALU.add)
                if pipe is not None:
                    pipe.step()
        while pipe is not None and not pipe.done():
            pipe.step()
        P.sb_top = m0

    def mixer1(hf):
        phase('m1_%d' % hf)
        rmsnorm_half(hf, PV_NM + 8)
        m0 = P.sb_top
        wb = WBuf("l", cast="dve", size=2048)
        wab = P.sb("l_wa", [128, 8, 128], BF16)
        wxb = P.sb("l_wx", [128, 8, 128], BF16)
        w1 = lw_in.ap()
        lspec = [([kview(w1[:, 1024 + c * 128:1024 + (c + 1) * 128]), kview(w1[:, c * 128:(c + 1) * 128])],
                  "lwin_%d" % c) for c in range(8)]
        ospec = lambda wd, key: [([kview(wd[:, b * 256:(b + 1) * 256])], "%s_%d" % (key, b)) for b in range(4)]
        bfa = wb.load([lw_a.ap().rearrange("n i j -> i n j")], 8, "lwa", hf)
        bfx = wb.load([lw_x.ap().rearrange("n i j -> i n j")], 8, "lwx", hf)
        P.copy("pool", wab[:], bfa)
        wb.prefetch(lspec[0], 8, hf)
        P.copy("pool", wxb[:], bfx)
        stg = [P.sb("l_st%d" % i, [128, 515], F32) for i in range(2)]
        NBX, NBG = 4, 4

        def bufs(name, dt=F32, n=NBX):
            return [P.sb("l_%s%d" % (name, i), [128, 512], dt) for i in range(n)]
        xc, ra, igu, a2, hs = (bufs(n_) for n_ in ("xc", "ra", "igu", "a2", "hs"))
        xcb = bufs("xcb", BF16)
        xg, x2z = bufs("xg", F32, NBG), bufs("x2z", F32, NBG)
        cnt = [0, 0]
        hs_of = {}

        def cons(bi, ci, tb, bk):
            c = bi
            tsl = slice(tb * 512, (tb + 1) * 512)
            if ci == 0:
                i = cnt[0] % NBX
                cnt[0] += 1
                hs_of[(c, tb)] = hs[i]
                sg = stg[tb % 2]
                W = 4
                wcol = lambda j: pv[:, PV_LCW + j * 8 + c:PV_LCW + j * 8 + c + 1]
                if tb == 0:
                    if hf == 0:
                        P.memset("pool", sg[:, 0:3], 0.0)
                    else:
                        P.copy("pool", sg[:, 0:3], lhalo[:, c, :])
                    hnext = stg[1][:, 0:3]
                else:
                    hnext = lhalo[:, c, :] if hf == 0 else None
                P.copy("act", sg[:, W - 1:W - 1 + 512], bk[:])
                P.act(xc[i][:], bk[:], AF.Identity, scale=wcol(W - 1), bias=pv[:, PV_LCB + c:PV_LCB + c + 1])
                yield
                for j in range(W - 1):
                    P.stt("dve", xc[i][:], sg[:, j:j + 512], wcol(j), xc[i][:], ALU.mult, ALU.add)
                P.copy("dve", xcb[i][:], xc[i][:])
                if hnext is not None:
                    P.copy("pool", hnext, sg[:, 512:512 + W - 1])
                yield
                b1 = nb(hold=True)
                P.mm(b1[:], wab[:, c, :], xcb[i][:])
                b2 = nb(hold=True)
                P.mm(b2[:], wxb[:, c, :], xcb[i][:])
                yield
                P.act(ra[i][:], b1[:], AF.Exp, scale=-1.0, bias=nlb[:, c:c + 1])
                P.act(igu[i][:], b2[:], AF.Exp, scale=-1.0, bias=nlb[:, 8 + c:9 + c])
                rel(b1)
                rel(b2)
                P.act(ra[i][:], ra[i][:], AF.Ln, bias=1.0)
                P.act(igu[i][:], igu[i][:], AF.Ln, bias=1.0)
                P.act(ra[i][:], ra[i][:], AF.Exp, scale=-1.0)
                P.act(igu[i][:], igu[i][:], AF.Exp, scale=-1.0)
                P.act(ra[i][:], ra[i][:], AF.Exp, scale=lsc[:, c:c + 1])
                P.act(a2[i][:], ra[i][:], AF.Square)
                P.act(a2[i][:], a2[i][:], AF.Ln, scale=-1.0, bias=1.0000001)
                P.act(a2[i][:], a2[i][:], AF.Exp, scale=0.5)
                yield
                P.tt("dve", igu[i][:], igu[i][:], xc[i][:], ALU.mult)
                P.tt("dve", igu[i][:], igu[i][:], a2[i][:], ALU.mult)
                init = 0.0 if (hf == 0 and tb == 0) else lstate[:, c:c + 1]
                rd = [ra[i][:], igu[i][:]] + ([] if isinstance(init, float) else [init])
                o_, a_, u_ = hs[i][:], ra[i][:], igu[i][:]
                P.add("dve", lambda e, o_=o_, a_=a_, u_=u_, init=init: e.tensor_tensor_scan(
                    o_, a_, u_, init, ALU.mult, ALU.add), rd, [o_])
                P.copy("dve", lstate[:, c:c + 1], hs[i][:, 511:512])
            else:
                i = cnt[1] % NBG
                cnt[1] += 1
                hsb = hs_of[(c, tb)]
                P.copy("act", xg[i][:], bk[:])
                yield
                P.tt("pool", x2z[i][:], xg[i][:], xg[i][:], ALU.mult)
                yield
                P.ts("dve", x2z[i][:], x2z[i][:], 0.044715, ALU.mult, 1.0, ALU.add)
                P.tt("dve", x2z[i][:], x2z[i][:], xg[i][:], ALU.mult)
                yield
                P.act(x2z[i][:], x2z[i][:], AF.Exp, scale=-1.5957691216057308)
                P.act(x2z[i][:], x2z[i][:], AF.Ln, bias=1.0)
                P.act(x2z[i][:], x2z[i][:], AF.Exp, scale=-1.0)
                yield
                P.tt("pool", x2z[i][:], x2z[i][:], xg[i][:], ALU.mult)
                yield
                P.tt("pool", yb[:, c, tsl], x2z[i][:], hsb[:], ALU.mult)
        proj_fm(wb, lspec, 8, lambda kc, tb: hn[:, kc, tb * 512:(tb + 1) * 512], cons, hf,
                after=(ospec(lw_out.ap(), "lwout")[0], 8))
        add_dump("yl%d" % hf, yb[:], [128, 8, TH])
        out_proj(wb, lw_out.ap(), hf, "lwout")
        P.sb_top = m0

    last = []

    class Pipe:
        def __init__(self, factories, nflight, gap):
            self.fac = list(factories)
            self.nflight, self.gap = nflight, gap
            self.active, self.k, self.stepno = [], 0, 0

        def done(self):
            return self.k >= len(self.fac) and not self.active

        def step(self):
            if self.k < len(self.fac) and len(self.active) < self.nflight and self.stepno >= self.k * self.gap:
                self.active.append(self.fac[self.k](self.k % self.nflight))
                self.k += 1
            run_pipe(self.active)
            self.stepno += 1

    def store_slots(n, ot_at=None):
        return [dict(sq=P.sb("so_sq%d" % i, [128, KC, 128], BF16), rstd=P.sb("so_r%d" % i, [128, 128], F32),
                     ot=P.sb("so_ot%d" % i, [128, D], F32, at=None if ot_at is None else ot_at + i * 4096))
                for i in range(n)]

    def store_chunk(tcg, B):
        tsl = slice(tcg * 128, (tcg + 1) * 128)
        hv_ = hT[:, :, tsl]
        P.act(B["sq"][:], hv_, AF.Square)
        yield
        bk = nb(hold=True)
        for c in range(KC):
            P.mm(bk[:, 0:128], ones_bf[:], B["sq"][:, c, :], start=(c == 0), stop=(c == KC - 1))
        yield
        P.act(B["rstd"][:], bk[:, 0:128], AF.Ln, scale=1.0 / D, bias=EPS)
        rel(bk)
        P.act(B["rstd"][:], B["rstd"][:], AF.Exp, scale=-0.5)
        yield
        P.tt("dve", hv_, hv_, B["rstd"][:].unsqueeze(1).to_broadcast([128, KC, 128]), ALU.mult)
        P.tt("dve", hv_, hv_, pv[:, PV_NFIN:PV_NFIN + KC].unsqueeze(2).to_broadcast([128, KC, 128]), ALU.mult)
        yield
        for g in range(2):
            bk2 = nb()
            for j in range(4):
                P.tr(bk2[:, j * 128:(j + 1) * 128], hT[:, g * 4 + j, tsl], ident)
            P.copy("dve" if g == 0 else "act", B["ot"][:, g * 512:(g + 1) * 512], bk2[:])
        last.append(P.dma(out_d.ap()[tcg * 128:(tcg + 1) * 128, :], B["ot"][:]))

    def store_pipe(hf, slots, gap=2):
        return Pipe([(lambda si, tcg=hf * 8 + k: store_chunk(tcg, slots[si])) for k in range(8)], len(slots), gap)

    def store(hf, norm):
        phase('store%d' % hf)
        m0 = P.sb_top
        o32 = P.sb("o32", [128, KC, 512], F32)
        ot = [P.sb("ot%d" % i, [128, D], F32) for i in range(2)]
        sq = P.sb("fn_sq", [128, KC, 512], BF16)
        rstd = P.sb("fn_rstd", [128, 512], F32)
        for tb in range(2):
            ts_ = slice(hf * TH + tb * 512, hf * TH + (tb + 1) * 512)
            if norm:
                for c in range(KC):
                    if c % 2 == 0:
                        P.act(sq[:, c, :], hT[:, c, ts_], AF.Square)
                    else:
                        P.tt("pool", sq[:, c, :], hT[:, c, ts_], hT[:, c, ts_], ALU.mult)
                bk = nb()
                for c in range(KC):
                    P.mm(bk[:], ones_bf[:], sq[:, c, :], start=(c == 0), stop=(c == KC - 1))
                P.act(rstd[:], bk[:], AF.Ln, scale=1.0 / D, bias=EPS)
                P.act(rstd[:], rstd[:], AF.Exp, scale=-0.5)
                for c in range(KC):
                    P.stt("dve", o32[:, c, :], hT[:, c, ts_], pv[:, PV_NFIN + c:PV_NFIN + c + 1], rstd[:],
                          ALU.mult, ALU.mult)
            for t4 in range(4):
                tcg = hf * 8 + tb * 4 + t4
                otb = ot[tcg % 2]
                for g in range(2):
                    bk = nb()
                    for j in range(4):
                        c = g * 4 + j
                        src = o32[:, c, t4 * 128:(t4 + 1) * 128] if norm else \
                            hT[:, c, tcg * 128:(tcg + 1) * 128]
                        P.tr(bk[:, j * 128:(j + 1) * 128], src, ident)
                    P.copy("dve" if g == 0 else "act", otb[:, g * 512:(g + 1) * 512], bk[:])
                last.append(P.dma(out_d.ap()[tcg * 128:(tcg + 1) * 128, :], otb[:]))
        P.sb_top = m0

    P.sb_top = SCR
    for hf in range(2):
        if stage >= 1:
            mixer0(hf)
        if stage >= 4:
            ffn(0, hf)
    if stage >= 7:
        mixer1(0)
        ffn(1, 0)
        mixer1(1)
        ffn(1, 1, inject=lambda slots: store_pipe(0, slots))
        phase('store1')
        m0_ = P.sb_top
        pipe = store_pipe(1, store_slots(4), gap=1)
        while not pipe.done():
            pipe.step()
        P.sb_top = m0_
    else:
        for hf in range(2):
            if stage >= 5:
                mixer1(hf)
            if stage >= 6:
                ffn(1, hf)
        for hf in range(2):
            store(hf, False)
    P.emit(final_wait_ops=last + list(dump_d.values()))
    return nc, P


_CACHE = {}


def prep_inputs(inp):
    cmat, rope = host_consts()
    pvec, hvec = host_pvec(inp)
    shared = dict(
        w_in0=np.ascontiguousarray(inp["ret_gdn_w_in"][0]), w_out0=np.ascontiguousarray(inp["ret_gdn_w_out"][0]),
        lw_in=np.ascontiguousarray(inp["lru_w_in"][0]), lw_a=np.ascontiguousarray(inp["lru_w_a"][0]),
        lw_x=np.ascontiguousarray(inp["lru_w_x"][0]), lw_out=np.ascontiguousarray(inp["lru_w_out"][0]),
        f_up=np.ascontiguousarray(inp["ffn_w_up"]), f_dn=np.ascontiguousarray(inp["ffn_w_down"]),
        pvec=pvec, hvec=hvec, cmat=cmat, rope=rope)
    return shared


def kernel(**inputs):
    inp = {k: np.asarray(v, dtype=np.float32) for k, v in inputs.items()}
    if "nc" not in _CACHE:
        _CACHE["nc"] = build()[0]
    nc = _CACHE["nc"]
    shared = prep_inputs(inp)
    n = 8
    in_maps = [dict(shared, x=np.ascontiguousarray(inp["x"][b])) for b in range(n)]
    res = run_bass_kernel_spmd(nc, in_maps, core_ids=list(range(n)))
    return np.stack([np.asarray(r["out"], dtype=np.float32) for r in res.results], axis=0)
```

```python
import numpy as np
import concourse.bass as bass
import concourse.mybir as mybir
from concourse.bass_utils import run_bass_kernel_spmd

F32 = mybir.dt.float32
BF16 = mybir.dt.bfloat16
AF = mybir.ActivationFunctionType
ALU = mybir.AluOpType
AX = mybir.AxisListType

ENG_ATTR = {"sync": "sync", "act": "scalar", "pool": "gpsimd", "dve": "vector", "pe": "tensor"}
ESIZE = {F32: 4, BF16: 2}
kindof = {}


def _esize(dt):
    return ESIZE[dt]


class Prog:
    def __init__(self, nc, n_dma_sems=40):
        self.nc = nc
        self.ops = []
        self.hist = {}
        self.base = {}
        self.n_dma_sems = n_dma_sems
        self.sb_top = 0

    def sb(self, name, shape, dtype, at=None):
        nbytes = int(np.prod(shape[1:])) * _esize(dtype)
        if at is None:
            at = (self.sb_top + 63) // 64 * 64
            self.sb_top = at + nbytes
        assert at + nbytes <= 229344 - 16512, (name, at, nbytes)
        t = self.nc.alloc_sbuf_tensor_at(name, list(shape), dtype, offset=16512 + at)
        self.base[t.name] = ("sb", at)
        return t

    def ps(self, name, shape, dtype):
        t = self.nc.alloc_psum_tensor(name, list(shape), dtype)
        self.base[t.name] = ("ps:" + name, 0)
        return t

    def dram(self, name, shape, dtype, kind):
        t = self.nc.dram_tensor(name, list(shape), dtype, kind=kind)
        self.base[t.name] = ("dr:" + name, 0)
        kindof[t.name] = kind
        return t

    def region(self, ap):
        t = ap.tensor
        name = t.name
        space, base = self.base[name]
        es = _esize(ap.dtype)
        if space.startswith("dr:"):
            if kindof.get(name) != "ExternalOutput":
                return (space, 0, 1, 0, 1)
            lo = hi = 0
            for st_, cnt_ in ap.ap:
                if st_ >= 0:
                    hi += st_ * (cnt_ - 1)
                else:
                    lo += st_ * (cnt_ - 1)
            return (space, 0, 1, (ap.offset + lo) * es, (ap.offset + hi + 1) * es)
        pstride = int(np.prod(t.shape[1:]))
        tes = _esize(t.dtype)
        off = ap.offset
        dims = list(ap.ap)
        p0 = off // pstride
        inoff = off % pstride
        pst, pcnt = dims[0]
        assert pst == pstride or pcnt == 1, (name, dims, pstride)
        lo = hi = 0
        for st, cnt in dims[1:]:
            if st >= 0:
                hi += st * (cnt - 1)
            else:
                lo += st * (cnt - 1)
        a0 = base + (inoff + lo) * es
        a1 = base + (inoff + hi + 1) * es
        return (space, p0, p0 + pcnt, a0, a1)

    def regions(self, ap):
        reg = self.region(ap)
        space = reg[0]
        if space != "sb":
            return [reg]
        free = [(st, cnt) for st, cnt in list(ap.ap)[1:] if cnt > 1]
        if len(free) < 2:
            return [reg]
        k = max(range(len(free)), key=lambda i: abs(free[i][0]))
        st0, cnt0 = free[k]
        inner = 1 + sum(abs(st) * (cnt - 1) for i, (st, cnt) in enumerate(free) if i != k)
        if st0 <= 0 or st0 < inner or cnt0 > 32:
            return [reg]
        es = _esize(ap.dtype)
        _, p0, p1, a0, _ = reg
        return [(space, p0, p1, a0 + j * st0 * es, a0 + j * st0 * es + inner * es) for j in range(cnt0)]

    PAGE = 2048

    def _conflicts(self, reg, is_write, opid, deps, eng, dma):
        space, p0, p1, a0, a1 = reg
        if space.startswith("ps:"):
            stt = self.hist.setdefault(space, {})
            other = [oid for e2, oid in stt.items() if e2 != eng]
            if other:
                deps.update(other)
                stt.clear()
            stt[eng] = opid
            return
        pages = self.hist.setdefault(space, {})
        ent_new = (p0, p1, a0, a1, is_write, opid)
        for pg in range(a0 // self.PAGE, (a1 - 1) // self.PAGE + 1):
            lst = pages.get(pg)
            if lst is None:
                pages[pg] = [ent_new]
                continue
            keep = []
            for ent in lst:
                q0, q1, b0, b1, w, oid = ent
                if oid == opid:
                    keep.append(ent)
                    continue
                if (not is_write and not w and not dma and q0 == p0 and q1 == p1 and b0 == a0 and b1 == a1
                        and self.ops[oid]["eng"] == eng and not self.ops[oid]["dma"]):
                    continue
                overlap = not (q1 <= p0 or p1 <= q0 or b1 <= a0 or a1 <= b0)
                if overlap and (w or is_write):
                    deps.add(oid)
                if is_write and q0 >= p0 and q1 <= p1 and b0 >= a0 and b1 <= a1:
                    continue
                keep.append(ent)
            keep.append(ent_new)
            pages[pg] = keep

    def add(self, eng, fn, reads=(), writes=(), dma=False):
        opid = len(self.ops)
        deps = set()
        self.ops.append(dict(eng=eng, fn=fn, deps=deps, dma=dma, id=opid))
        for ap in writes:
            for reg in self.regions(ap):
                self._conflicts(reg, True, opid, deps, eng, dma)
        for ap in reads:
            for reg in self.regions(ap):
                self._conflicts(reg, False, opid, deps, eng, dma)
        return opid

    def dma(self, out, in_, eng="sync"):
        return self.add(eng, lambda e: e.dma_start(out=out, in_=in_), [in_], [out], dma=True)

    def mm(self, out, lhsT, rhs, start=True, stop=True):
        return self.add("pe", lambda e: e.matmul(out, lhsT, rhs, start=start, stop=stop), [lhsT, rhs], [out])

    def tr(self, out, in_, ident):
        return self.add("pe", lambda e: e.transpose(out, in_, ident), [in_, ident], [out])

    def act(self, out, in_, func, bias=None, scale=None, eng="act"):
        kw = {}
        rd = [in_]
        if bias is not None:
            kw["bias"] = bias
            if not isinstance(bias, (int, float)):
                rd.append(bias)
        if scale is not None:
            kw["scale"] = scale
            if not isinstance(scale, (int, float)):
                rd.append(scale)
        return self.add(eng, lambda e: e.activation(out, in_, func, **kw), rd, [out])

    def tt(self, eng, out, in0, in1, op):
        return self.add(eng, lambda e: e.tensor_tensor(out, in0, in1, op), [in0, in1], [out])

    def ts(self, eng, out, in0, s1, op0, s2=None, op1=None):
        rd = [in0] + [s for s in (s1, s2) if s is not None and not isinstance(s, (int, float))]
        if op1 is None:
            return self.add(eng, lambda e: e.tensor_scalar(out, in0, s1, None, op0), rd, [out])
        return self.add(eng, lambda e: e.tensor_scalar(out, in0, s1, s2, op0, op1), rd, [out])

    def stt(self, eng, out, in0, scalar, in1, op0, op1):
        rd = [in0, in1] + ([] if isinstance(scalar, (int, float)) else [scalar])
        return self.add(eng, lambda e: e.scalar_tensor_tensor(out, in0, scalar, in1, op0, op1), rd, [out])

    def copy(self, eng, out, in_):
        if eng == "act":
            return self.add(eng, lambda e: e.copy(out, in_), [in_], [out])
        return self.add(eng, lambda e: e.tensor_copy(out, in_), [in_], [out])

    def memset(self, eng, out, val):
        return self.add(eng, lambda e: e.memset(out, val), [], [out])

    def emit(self, final_wait_ops=()):
        nc = self.nc
        ops = self.ops
        used = set()
        for o in ops:
            best = {}
            need = []
            for d in o["deps"]:
                dop = ops[d]
                if dop["dma"]:
                    need.append(d)
                    continue
                if dop["eng"] == "pe" and o["eng"] == "pe" and not o["dma"]:
                    continue
                if best.get(dop["eng"], -1) < d:
                    best[dop["eng"]] = d
            need += list(best.values())
            o["need"] = need
            used.update(need)
        for d in final_wait_ops:
            used.add(d)
        engs = list(ENG_ATTR.keys())
        import contextlib
        with contextlib.ExitStack() as st:
            esem = {e: st.enter_context(nc.semaphore("s_" + e)) for e in engs}
            dsems = [st.enter_context(nc.semaphore("d%d" % i)) for i in range(self.n_dma_sems)]
            cnt = {e: 0 for e in engs}
            dval = [0] * self.n_dma_sems
            dnext = 0
            for o in ops:
                if o["dma"]:
                    k = dnext
                    dnext = (dnext + 1) % self.n_dma_sems
                    o["prev_tok"] = ("d", k, dval[k])
                    dval[k] += 16
                    o["tok"] = ("d", k, dval[k])
                elif o["id"] in used:
                    cnt[o["eng"]] += 1
                    o["tok"] = ("e", o["eng"], cnt[o["eng"]])
                else:
                    o["tok"] = None
            seen = {e: {} for e in engs}

            def semof(tok):
                return esem[tok[1]] if tok[0] == "e" else dsems[tok[1]]

            plans = {e: [] for e in engs}
            for o in ops:
                e = o["eng"]
                sn = seen[e]
                waits = []
                need = []
                for d in o["need"]:
                    need.append(ops[d])
                if o["dma"] and o["prev_tok"][2] > 0:
                    need.append(dict(tok=o["prev_tok"], clock={}))
                for dop in need:
                    tok = dop["tok"]
                    key = tok[:2]
                    if sn.get(key, 0) >= tok[2]:
                        continue
                    waits.append(tok)
                    sn[key] = tok[2]
                    for k2, v2 in dop.get("clock", {}).items():
                        if sn.get(k2, 0) < v2:
                            sn[k2] = v2
                wm = {}
                for tok in waits:
                    wm[tok[:2]] = max(wm.get(tok[:2], 0), tok[2])
                if o["tok"] is not None:
                    o["clock"] = dict(sn)
                    if o["tok"][0] == "e":
                        o["clock"][o["tok"][:2]] = o["tok"][2]
                plans[e].append((o, wm))
            self.n_waits = sum(len(w) for e in engs for _, w in plans[e])
            final = [ops[d]["tok"] for d in final_wait_ops]
            with nc.Block() as block:
                def mk(e):
                    def body(engine):
                        for o, wm in plans[e]:
                            for key, v in wm.items():
                                engine.wait_ge(semof(key + (0,)), v)
                            inst = o["fn"](engine)
                            if o["tok"] is not None:
                                tok = o["tok"]
                                inst.then_inc(semof(tok), 16 if tok[0] == "d" else 1)
                        if e == "sync":
                            for tok in final:
                                engine.wait_ge(semof(tok), tok[2])
                    return body
                for e in engs:
                    getattr(block, ENG_ATTR[e])(mk(e))


T = 2048
TH = 1024
D = 1024
KC = 8
DFF = 2816
EPS = 1e-6
NPV = 512
PV_NM, PV_NF, PV_NFIN, PV_GCW, PV_GOG, PV_LCW, PV_LCB, PV_LBA, PV_LBX, PV_LAM, PV_FCW, PV_FCB = (
    0, 16, 32, 40, 88, 89, 121, 129, 137, 145, 153, 417)
CM_ID, CM_U, CM_SL, CM_PERM, CM_DMT, CM_QDC, CM_KTD, CM_CDC = 0, 128, 256, 384, 512, 1024, 1536, 1540
NCM = 1544
DK_SCALE = 128.0 ** -0.5


def host_consts():
    f8 = np.float64
    idx = np.arange(128)
    cm = np.zeros((128, NCM), np.float64)
    cm[:, CM_ID:CM_ID + 128] = np.eye(128)
    cm[:, CM_U:CM_U + 128] = (idx[:, None] <= idx[None, :])
    cm[:, CM_SL:CM_SL + 128] = (idx[:, None] > idx[None, :])
    perm = np.zeros((128, 128))
    perm[idx, (idx + 64) % 128] = 1.0
    cm[:, CM_PERM:CM_PERM + 128] = perm
    lg = np.log1p(-np.exp2(-5.0 - np.arange(4, dtype=f8)))
    for h in range(4):
        rel = idx[None, :] - idx[:, None]
        m = np.where(rel >= 0, np.exp(lg[h] * np.maximum(rel, 0)), 0.0) * DK_SCALE
        cm[:, CM_DMT + h * 128:CM_DMT + (h + 1) * 128] = m
        cm[:, CM_QDC + h * 128:CM_QDC + (h + 1) * 128] = np.exp(lg[h] * (idx + 1.0))[None, :]
        cm[:, CM_KTD + h] = np.exp(lg[h] * (127 - idx)) * DK_SCALE
        cm[:, CM_CDC + h] = np.exp(lg[h] * 128.0)
    half = 64
    inv_freq = (np.float32(10000.0) ** (-np.arange(half, dtype=np.float32) / np.float32(half))).astype(np.float32)
    ang = (np.arange(T, dtype=np.float32)[:, None] * inv_freq[None, :]).astype(np.float32)
    cos = np.cos(ang.astype(f8)).T
    sin = np.sin(ang.astype(f8)).T
    rope = np.zeros((128, 2 * T), np.float64)
    rope[0:64, 0:T] = cos
    rope[64:128, 0:T] = cos
    rope[0:64, T:2 * T] = -sin
    rope[64:128, T:2 * T] = sin
    return cm.astype(np.float32), rope.astype(np.float32)


def host_pvec(inp):
    pv = np.zeros((128, NPV), np.float32)

    def put(col, vec):
        n = vec.shape[0] // 128
        pv[:, col:col + n] = vec.reshape(n, 128).T

    for l in range(2):
        put(PV_NM + l * 8, inp["norm_mix"][l])
        put(PV_NF + l * 8, inp["norm_ffn"][l])
    put(PV_NFIN, inp["norm_final"])
    for j in range(4):
        put(PV_GCW + j * 12, inp["gdn_conv_w"][0, j])
        put(PV_LCW + j * 8, inp["lru_conv_w"][0, j])
    put(PV_GOG, inp["gdn_out_gain"][0])
    put(PV_LCB, inp["lru_conv_b"][0])
    put(PV_LBA, inp["lru_b_a"][0])
    put(PV_LBX, inp["lru_b_x"][0])
    put(PV_LAM, inp["lru_lambda"][0])
    for l in range(2):
        for j in range(3):
            put(PV_FCW + l * 132 + j * 44, inp["ffn_conv_w"][l, j])
        put(PV_FCB + l * 44, inp["ffn_conv_b"][l])
    hv = np.zeros((128, 8), np.float32)
    hv[:, 0:4] = inp["gdn_dt_bias"][0][None, :]
    hv[:, 4:8] = inp["gdn_a_log"][0][None, :]
    return pv, hv


def build(stage=99, dump=None):
    nc = bass.Bass("TRN2", target_bir_lowering=False)
    P = Prog(nc)
    IN = "ExternalInput"
    x_d = P.dram("x", [T, D], F32, IN)
    w_in0 = P.dram("w_in0", [D, 4104], F32, IN)
    w_out0 = P.dram("w_out0", [D, D], F32, IN)
    lw_in = P.dram("lw_in", [D, 2048], F32, IN)
    lw_a = P.dram("lw_a", [8, 128, 128], F32, IN)
    lw_x = P.dram("lw_x", [8, 128, 128], F32, IN)
    lw_out = P.dram("lw_out", [D, D], F32, IN)
    f_up = P.dram("f_up", [2, D, 2 * DFF], F32, IN)
    f_dn = P.dram("f_dn", [2, DFF, D], F32, IN)
    pvec_d = P.dram("pvec", [128, NPV], F32, IN)
    hvec_d = P.dram("hvec", [128, 8], F32, IN)
    cmat_d = P.dram("cmat", [128, NCM], F32, IN)
    rope_d = P.dram("rope", [128, 2 * T], F32, IN)
    out_d = P.dram("out", [T, D], F32, "ExternalOutput")
    dump_d = {}

    banks = [P.ps("bk%d" % i, [128, 512], F32) for i in range(8)]
    st = dict(b=0, q=0)

    held = set()

    def nb(hold=False):
        for _ in range(8):
            i = st["b"] % 8
            st["b"] += 1
            if i not in held:
                if hold:
                    held.add(i)
                return banks[i]
        raise AssertionError("all PSUM banks held")

    def rel(b):
        held.discard(banks.index(b))

    hT = P.sb("hT", [128, KC, T], F32)
    pv = P.sb("pv", [128, NPV], F32)
    hv = P.sb("hv", [128, 8], F32)
    cm = P.sb("cm", [128, NCM], F32)
    ident = cm[:, CM_ID:CM_ID + 128]
    cb = P.sb("cb", [128, 512], BF16)
    ident_bf, U_bf, SL_bf, perm_bf = (cb[:, i * 128:(i + 1) * 128] for i in range(4))
    ones_bf = P.sb("ones_bf", [128, 128], BF16)
    MiT = P.sb("MiT", [128, 128], F32)
    dmaskT = cm[:, CM_DMT:CM_DMT + 512].rearrange("p (h i) -> p h i", h=4)
    qdec_c = cm[:, CM_QDC:CM_QDC + 512].rearrange("p (h i) -> p h i", h=4)
    Ms = cm[:, CM_SL:CM_SL + 128]
    hn = P.sb("hn", [128, KC, TH], BF16)
    yb = P.sb("yb", [128, KC, TH], BF16)
    Sr32 = P.sb("Sr32", [128, 4, 128], F32)
    Srb = P.sb("Srb", [128, 4, 128], BF16)
    Sg32 = P.sb("Sg32", [128, 4, 128], F32)
    Sgb = P.sb("Sgb", [128, 4, 128], BF16)
    ghalo = P.sb("ghalo", [128, 12, 3], F32)
    fhalo = P.sb("fhalo", [128, 44, 2], F32)
    lhalo = P.sb("lhalo", [128, 8, 3], F32)
    lstate = P.sb("lstate", [128, 8], F32)
    lsc = P.sb("lsc", [128, 16], F32)
    negA = P.sb("negA", [128, 4], F32)
    nlb = P.sb("nlb", [128, 16], F32)
    SCR = P.sb_top

    P.dma(pv[:], pvec_d.ap())
    P.dma(hv[:], hvec_d.ap())
    P.dma(cm[:], cmat_d.ap())
    P.copy("dve", cb[:], cm[:, 0:512])
    P.memset("pool", ones_bf[:], 1.0)
    P.ts("dve", MiT[:], cm[:, CM_U:CM_U + 128], DK_SCALE, ALU.mult)
    P.act(negA[:], hv[:, 4:8], AF.Exp)
    P.ts("dve", negA[:], negA[:], -1.0, ALU.mult)
    P.act(lsc[:, 0:8], pv[:, PV_LAM:PV_LAM + 8], AF.Exp, scale=-1.0)
    P.act(lsc[:, 0:8], lsc[:, 0:8], AF.Ln, bias=1.0)
    P.ts("dve", lsc[:, 8:16], lsc[:, 0:8], -16.0, ALU.mult)
    P.ts("dve", lsc[:, 0:8], lsc[:, 0:8], -8.0, ALU.mult)
    P.ts("dve", nlb[:, 0:8], pv[:, PV_LBA:PV_LBA + 8], -1.0, ALU.mult)
    P.ts("dve", nlb[:, 8:16], pv[:, PV_LBX:PV_LBX + 8], -1.0, ALU.mult)
    for t_ in (Sr32, Srb, Sg32, Sgb, lstate):
        P.memset("pool", t_[:], 0.0)

    P.phases = []

    def phase(name):
        P.phases.append((name, sum(1 for o in P.ops if o["eng"] == "pe")))

    def add_dump(name, ap, shape):
        if dump is None or name not in dump:
            return
        d = P.dram("dump_" + name, list(shape), ap.dtype, "ExternalOutput")
        dump_d[name] = P.dma(d.ap(), ap)

    P.sb_top = SCR
    xt = [P.sb("xt%d" % i, [128, D], F32) for i in range(2)]
    def load_x(tc, bufs=None):
        xb = (bufs or xt)[tc % 2]
        P.dma(xb[:], x_d.ap()[tc * 128:(tc + 1) * 128, :])
        for g in range(2):
            bk = nb()
            for j in range(4):
                c = g * 4 + j
                P.tr(bk[:, j * 128:(j + 1) * 128], xb[:, c * 128:(c + 1) * 128], ident)
            P.copy("dve" if g == 0 else "act", hT[:, g * 4:(g + 1) * 4, tc * 128:(tc + 1) * 128],
                   bk[:].rearrange("p (j t) -> p j t", j=4))
    for tc in range(8):
        load_x(tc)

    def rmsnorm_half(hf, gcol):
        mark = P.sb_top
        sq = P.sb("rn_sq", [128, KC, 512], BF16)
        rstd = P.sb("rn_rstd", [128, 512], F32)
        for tb in range(2):
            ts_ = slice(hf * TH + tb * 512, hf * TH + (tb + 1) * 512)
            for c in range(KC):
                if c % 2 == 0:
                    P.act(sq[:, c, :], hT[:, c, ts_], AF.Square)
                else:
                    P.tt("pool", sq[:, c, :], hT[:, c, ts_], hT[:, c, ts_], ALU.mult)
            bk = nb()
            for c in range(KC):
                P.mm(bk[:], ones_bf[:], sq[:, c, :], start=(c == 0), stop=(c == KC - 1))
            P.act(rstd[:], bk[:], AF.Ln, scale=1.0 / D, bias=EPS)
            P.act(rstd[:], rstd[:], AF.Exp, scale=-0.5)
            for c in range(KC):
                P.stt("dve", hn[:, c, tb * 512:(tb + 1) * 512], hT[:, c, ts_], pv[:, gcol + c:gcol + c + 1],
                      rstd[:], ALU.mult, ALU.mult)
        P.sb_top = mark

    wcache = {}

    class WBuf:
        def __init__(self, tag, cast="act", size=2816):
            self.st = [P.sb("wst%s%d" % (tag, i), [128, size], F32) for i in range(2)]
            self.bf = [P.sb("wbf%s%d" % (tag, i), [128, size], BF16) for i in range(2)]
            self.i = 0
            self.cast = cast
            self.pend = None

        def get(self, spec, kcn, hf):
            if self.pend is not None and self.pend[0] == spec[1]:
                bfv = self.pend[1]
                self.pend = None
                return bfv
            return self.load(spec[0], kcn, spec[1], hf)

        def prefetch(self, spec, kcn, hf):
            self.pend = (spec[1], self.load(spec[0], kcn, spec[1], hf))

        def load(self, srcs, kcn, key, hf):
            ntot = sum(s_.shape[2] for s_ in srcs)
            tot = kcn * ntot
            i = self.i % 2
            self.i += 1
            bfv = self.bf[i][:, 0:tot].rearrange("p (k n) -> p k n", k=kcn)
            if hf == 1 and key in wcache:
                P.dma(self.bf[i][:, 0:tot], wcache[key].ap())
                return bfv
            stv = self.st[i][:, 0:tot].rearrange("p (k n) -> p k n", k=kcn)
            off = 0
            for s_ in srcs:
                n = s_.shape[2]
                P.dma(stv[:, :, off:off + n], s_)
                off += n
            P.copy(self.cast, self.bf[i][:, 0:tot], self.st[i][:, 0:tot])
            if hf == 0:
                wcache[key] = P.dram("wc_" + key, [128, tot], BF16, "Internal")
                P.dma(wcache[key].ap(), self.bf[i][:, 0:tot], eng="act")
            return bfv

    def kview(ap2):
        return ap2.rearrange("(k p) n -> p k n", p=128)

    def run_pipe(active):
        for g in list(reversed(active)):
            try:
                next(g)
            except StopIteration:
                active.remove(g)

    def proj_fm(wb, specs, kcn, rhs_fn, consume, hf, after=None):
        active = []
        nxt = wb.get(specs[0], kcn, hf)
        for bi in range(len(specs)):
            bfv = nxt
            if bi + 1 < len(specs):
                nxt = wb.load(specs[bi + 1][0], kcn, specs[bi + 1][1], hf)
            elif after is not None:
                wb.prefetch(after[0], after[1], hf)
            nch = bfv.shape[2] // 128
            for tb in range(2):
                for ci in range(nch):
                    bk = nb()
                    for kc in range(kcn):
                        P.mm(bk[:], bfv[:, kc, ci * 128:(ci + 1) * 128], rhs_fn(kc, tb),
                             start=(kc == 0), stop=(kc == kcn - 1))
                    g = consume(bi, ci, tb, bk)
                    if g is not None:
                        active.append(g)
                    run_pipe(active)
        while active:
            run_pipe(active)

    def conv_block(bk, stage, W, wcol, bias, acc, halo_next):
        P.copy("act", stage[:, W - 1:W - 1 + 512], bk[:])
        if bias is None:
            P.act(acc, bk[:], AF.Copy, scale=wcol(W - 1))
        else:
            P.act(acc, bk[:], AF.Identity, scale=wcol(W - 1), bias=bias)
        for j in range(W - 1):
            P.stt("dve", acc, stage[:, j:j + 512], wcol(j), acc, ALU.mult, ALU.add)
        if halo_next is not None:
            P.copy("pool", halo_next, stage[:, 512:512 + W - 1])

    def v4(bank):
        return bank[:].rearrange("p (h i) -> p h i", h=4)

    def v4b(bank):
        return bank[:].bitcast(BF16)[:, 0:512].rearrange("p (h i) -> p h i", h=4)

    def bc4(ap2):
        return ap2.unsqueeze(2).to_broadcast([128, 4, 128])

    def out_norm(bo, gain, gate, yout, tmp):
        sq, r, t = tmp
        P.act(sq[:], bo[:], AF.Square)
        bs = nb()
        P.mm(bs[:], ones_bf[:], sq[:])
        P.act(r[:], bs[:], AF.Ln, scale=1.0 / 128, bias=EPS)
        P.act(r[:], r[:], AF.Exp, scale=-0.5)
        if gain is None:
            P.tt("dve", t[:], bo[:], r[:], ALU.mult)
        else:
            P.stt("dve", t[:], bo[:], gain, r[:], ALU.mult, ALU.mult)
        P.tt("pool", yout, t[:].rearrange("p (h i) -> p h i", h=4), gate, ALU.mult)

    def out_norm_g(bo, gain, gate, yout, tmp):
        sq, r, t = tmp
        P.act(sq, bo[:], AF.Square)
        yield
        bs = nb()
        P.mm(bs[:], ones_bf[:], sq)
        P.act(r, bs[:], AF.Ln, scale=1.0 / 128, bias=EPS)
        P.act(r, r, AF.Exp, scale=-0.5)
        yield
        if gain is None:
            P.tt("dve", t, bo[:], r, ALU.mult)
        else:
            P.stt("dve", t, bo[:], gain, r, ALU.mult, ALU.mult)
        rel(bo)
        yield
        P.tt("pool", yout, t.rearrange("p (h i) -> p h i", h=4), gate, ALU.mult)

    def out_proj(wb, wd, hf, key):
        def consume(bi, ci, tb, bk):
            dc = bi * 2 + ci
            ts_ = slice(hf * TH + tb * 512, hf * TH + (tb + 1) * 512)
            P.tt("dve", hT[:, dc, ts_], hT[:, dc, ts_], bk[:], ALU.add)
        proj_fm(wb, [([kview(wd[:, b * 256:(b + 1) * 256])], "%s_%d" % (key, b)) for b in range(4)], 8,
                lambda kc, tb: yb[:, kc, tb * 512:(tb + 1) * 512], consume, hf)

    def mixer0(hf):
        w = w_in0.ap()
        phase('m0_norm%d' % hf)
        rmsnorm_half(hf, PV_NM)
        phase('m0_retproj%d' % hf)
        m0 = P.sb_top
        qr = P.sb("qr", [128, 4, TH], BF16)
        kr = P.sb("kr", [128, 4, TH], BF16)
        vtok = P.sb("vtok", [128, 8, 512], BF16)
        gr = P.sb("gr", [128, 4, TH], BF16)
        m1 = P.sb_top
        wb = WBuf("a", cast="dve", size=2048)
        ropec = P.sb("ropec", [128, TH], F32)
        ropes = P.sb("ropes", [128, TH], F32)
        P.dma(ropec[:], rope_d.ap()[:, hf * TH:(hf + 1) * TH])
        P.dma(ropes[:], rope_d.ap()[:, T + hf * TH:T + (hf + 1) * TH])
        NBUF = 4
        xbf = [P.sb("xbf%d" % i, [128, 512], BF16) for i in range(NBUF)]
        t1 = [P.sb("t1_%d" % i, [128, 512], F32) for i in range(NBUF)]
        t2 = [P.sb("t2_%d" % i, [128, 512], F32) for i in range(NBUF)]
        cnt = [0]
        hn_rhs = lambda kc, tb: hn[:, kc, tb * 512:(tb + 1) * 512]

        def wspec(c0, n=256):
            return ([kview(w[:, c0:c0 + n])], "win0_%d" % c0)

        def cons_qk(bi, ci, tb, bk):
            cc = bi * 2 + ci
            dst = qr if cc < 4 else kr
            i = cnt[0] % NBUF
            cnt[0] += 1
            tsl = slice(tb * 512, (tb + 1) * 512)
            P.copy("act", xbf[i][:], bk[:])
            P.tt("dve", t1[i][:], bk[:], ropec[:, tsl], ALU.mult)
            yield
            b2 = nb(hold=True)
            P.mm(b2[:], perm_bf, xbf[i][:])
            yield
            P.tt("dve", t2[i][:], b2[:], ropes[:, tsl], ALU.mult)
            rel(b2)
            P.tt("pool", dst[:, cc % 4, tsl], t1[i][:], t2[i][:], ALU.add)
        proj_fm(wb, [wspec(b * 256) for b in range(4)], 8, hn_rhs, cons_qk, hf, after=(wspec(1024), 8))
        nxt = wb.get(wspec(1024), 8, hf)
        for blk in range(2):
            bfv = nxt
            if blk == 0:
                nxt = wb.load(wspec(1280)[0], 8, wspec(1280)[1], hf)
            else:
                wb.prefetch(wspec(1536), 8, hf)
            for tc in range(8):
                bk = nb()
                for kc in range(8):
                    P.mm(bk[:, 0:256], hn[:, kc, tc * 128:(tc + 1) * 128], bfv[:, kc, :],
                         start=(kc == 0), stop=(kc == 7))
                P.copy("act", vtok[:, tc, blk * 256:(blk + 1) * 256], bk[:, 0:256])

        def cons_g(dst):
            def f(bi, ci, tb, bk):
                P.act(dst[:, bi * 2 + ci, tb * 512:(tb + 1) * 512], bk[:], AF.Silu)
            return f
        proj_fm(wb, [wspec(1536 + b * 256) for b in range(2)], 8, hn_rhs, cons_g(gr), hf)
        add_dump("qr%d" % hf, qr[:], [128, 4, TH])
        add_dump("kr%d" % hf, kr[:], [128, 4, TH])
        add_dump("vtok%d" % hf, vtok[:], [128, 8, 512])
        phase('m0_retrec%d' % hf)
        P.sb_top = m1
        ktd_bc = bc4(cm[:, CM_KTD:CM_KTD + 4])
        cdc_bc = bc4(cm[:, CM_CDC:CM_CDC + 4])
        rslots = []
        for sl in range(3):
            rslots.append(dict(
                ktb=P.sb("r%d_kt" % sl, [128, 4, 128], BF16), smb=P.sb("r%d_sm" % sl, [128, 4, 128], BF16),
                qdb=P.sb("r%d_qd" % sl, [128, 4, 128], BF16), stmp=P.sb("r%d_stmp" % sl, [128, 4, 128], F32),
                ntmp=(P.sb("n%d_sq" % sl, [128, 512], BF16)[:], P.sb("n%d_r" % sl, [128, 512], F32)[:],
                      P.sb("n%d_t" % sl, [128, 512], F32)[:])))

        def ret_chunk(n, B):
            cs = slice(n * 128, (n + 1) * 128)
            ktb, smb, qdb, stmp = B["ktb"], B["smb"], B["qdb"], B["stmp"]
            b1 = nb()
            for h in range(4):
                P.tr(v4b(b1)[:, h, :], kr[:, h, cs], ident_bf)
            b2 = nb()
            for h in range(4):
                P.mm(v4(b2)[:, h, :], kr[:, h, cs], qr[:, h, cs])
            P.tt("pool", qdb[:], qr[:, :, cs], qdec_c, ALU.mult)
            P.tt("dve", ktb[:], v4b(b1), ktd_bc, ALU.mult)
            P.tt("dve", smb[:], v4(b2), dmaskT, ALU.mult)
            yield
            b3 = nb(hold=True)
            for h in range(4):
                P.mm(v4(b3)[:, h, :], ktb[:, h, :], vtok[:, n, h * 128:(h + 1) * 128])
            b4 = nb(hold=True)
            for h in range(4):
                P.mm(v4(b4)[:, h, :], vtok[:, n, h * 128:(h + 1) * 128], smb[:, h, :], start=True, stop=False)
                P.mm(v4(b4)[:, h, :], Srb[:, h, :], qdb[:, h, :], start=False, stop=True)
            P.tt("pool", stmp[:], Sr32[:], cdc_bc, ALU.mult)
            yield
            P.tt("dve", Sr32[:], stmp[:], v4(b3), ALU.add)
            rel(b3)
            P.copy("act", Srb[:], Sr32[:])
            yield from out_norm_g(b4, None, gr[:, :, cs], yb[:, 0:4, cs], B["ntmp"])

        xq = list(range(8, 16)) if hf == 0 else []
        xt2 = [P.sb("xt2_%d" % i, [128, D], F32) for i in range(2)] if hf == 0 else None
        active = []
        nstart = 0
        step = 0
        while nstart < 8 or active:
            if nstart < 8 and len(active) < 3 and step >= nstart * 2:
                active.append(ret_chunk(nstart, rslots[nstart % 3]))
                nstart += 1
            run_pipe(active)
            step += 1
            if xq and step % 2 == 0:
                load_x(xq.pop(0), xt2)
        while xq:
            load_x(xq.pop(0), xt2)
        add_dump("yr%d" % hf, yb[:, 0:4, :], [128, 4, TH])
        if stage < 2:
            P.sb_top = m0
            return
        phase('m0_gdnproj%d' % hf)
        P.sb_top = m0
        qd = P.sb("qd", [128, 4, TH], BF16)
        kd = P.sb("kd", [128, 4, TH], BF16)
        vd = P.sb("vd", [128, 4, TH], BF16)
        gd = yb[:, 4:8, :]
        tm_pre = P.sb("tm_pre", [128, 8, 8], F32)
        m1 = P.sb_top
        wb = WBuf("b", cast="dve", size=2048)
        stg = [[P.sb("g_st%d_%d" % (ci, i), [128, 515], F32) for i in range(2)] for ci in range(2)]
        NBUF = 7
        acc = [P.sb("g_acc%d" % i, [128, 512], F32) for i in range(NBUF)]
        sqv_ = [P.sb("g_sq%d" % i, [128, 512], BF16) for i in range(3)]
        rv_ = [P.sb("g_r%d" % i, [128, 512], F32) for i in range(5)]
        cnt = [0]

        def cons_conv(bi, ci, tb, bk):
            cc = bi * 2 + ci
            i = cnt[0] % NBUF
            sqv = {i: sqv_[cnt[0] % 3]}
            rv = {i: rv_[cnt[0] % 5]}
            cnt[0] += 1
            tsl = slice(tb * 512, (tb + 1) * 512)
            sg = stg[ci][tb % 2]
            W = 4
            wcol = lambda j: pv[:, PV_GCW + j * 12 + cc:PV_GCW + j * 12 + cc + 1]
            if tb == 0:
                if hf == 0:
                    P.memset("pool", sg[:, 0:3], 0.0)
                else:
                    P.copy("pool", sg[:, 0:3], ghalo[:, cc, :])
                hnext = stg[ci][1][:, 0:3]
            else:
                hnext = ghalo[:, cc, :] if hf == 0 else None
            P.copy("act", sg[:, W - 1:W - 1 + 512], bk[:])
            P.act(acc[i][:], bk[:], AF.Copy, scale=wcol(W - 1))
            yield
            for j in range(W - 1):
                P.stt("dve", acc[i][:], sg[:, j:j + 512], wcol(j), acc[i][:], ALU.mult, ALU.add)
            if hnext is not None:
                P.copy("pool", hnext, sg[:, 512:512 + W - 1])
            yield
            P.act(rv[i][:], acc[i][:], AF.Exp, scale=-1.0)
            P.act(rv[i][:], rv[i][:], AF.Ln, bias=1.0)
            P.act(rv[i][:], rv[i][:], AF.Exp, scale=-1.0)
            yield
            if cc >= 8:
                P.tt("dve", vd[:, cc - 8, tsl], acc[i][:], rv[i][:], ALU.mult)
                return
            dst = qd if cc < 4 else kd
            P.tt("dve", acc[i][:], acc[i][:], rv[i][:], ALU.mult)
            P.tt("pool", sqv[i][:], acc[i][:], acc[i][:], ALU.mult)
            yield
            b2 = nb(hold=True)
            P.mm(b2[:], ones_bf[:], sqv[i][:])
            yield
            P.act(rv[i][:], b2[:], AF.Ln, bias=EPS)
            rel(b2)
            P.act(rv[i][:], rv[i][:], AF.Exp, scale=-0.5)
            yield
            P.tt("dve", dst[:, cc % 4, tsl], acc[i][:], rv[i][:], ALU.mult)
        proj_fm(wb, [wspec(2048 + b * 256) for b in range(6)], 8, hn_rhs, cons_conv, hf, after=(wspec(3584), 8))
        proj_fm(wb, [wspec(3584 + b * 256) for b in range(2)], 8, hn_rhs, cons_g(gd), hf, after=(wspec(4096, 8), 8))
        bfv = wb.get(wspec(4096, 8), 8, hf)
        bk = nb()
        for tc in range(8):
            for kc in range(8):
                P.mm(bk[:, tc * 8:(tc + 1) * 8], hn[:, kc, tc * 128:(tc + 1) * 128], bfv[:, kc, :],
                     start=(kc == 0), stop=(kc == 7))
        P.copy("dve", tm_pre[:], bk[:, 0:64].rearrange("p (n c) -> p n c", c=8))
        add_dump("qd%d" % hf, qd[:], [128, 4, TH])
        add_dump("kd%d" % hf, kd[:], [128, 4, TH])
        add_dump("vd%d" % hf, vd[:], [128, 4, TH])
        add_dump("tmpre%d" % hf, tm_pre[:], [128, 8, 8])
        P.sb_top = m1
        phase('m0_gdnrec%d' % hf)
        def tm(name, dt=F32):
            return P.sb("tm_" + name, [128, 8, 4], dt)
        beta, nbeta, gtk, gc, gmg, egc, bge, ekt, tmp1 = (tm(n_) for n_ in
            ("beta", "nbeta", "g", "gc", "gmg", "egc", "bge", "ekt", "tmp1"))
        ghi, glo, gchi, gclo = (tm(n_, BF16) for n_ in ("ghi", "glo", "gchi", "gclo"))
        P.act(beta[:], tm_pre[:, :, 0:4], AF.Sigmoid)
        P.ts("dve", nbeta[:], beta[:], -1.0, ALU.mult)
        P.tt("dve", tmp1[:], tm_pre[:, :, 4:8], hv[:, 0:4].unsqueeze(1).to_broadcast([128, 8, 4]), ALU.add)
        P.act(tmp1[:], tmp1[:], AF.Exp)
        P.act(tmp1[:], tmp1[:], AF.Ln, bias=1.0)
        P.tt("dve", gtk[:], tmp1[:], negA[:].unsqueeze(1).to_broadcast([128, 8, 4]), ALU.mult)
        P.copy("dve", ghi[:], gtk[:])
        P.tt("dve", glo[:], gtk[:], ghi[:], ALU.subtract)
        bk = nb()
        for n in range(8):
            P.mm(bk[:, n * 4:(n + 1) * 4], U_bf, ghi[:, n, :], start=True, stop=False)
            P.mm(bk[:, n * 4:(n + 1) * 4], U_bf, glo[:, n, :], start=False, stop=True)
        for n in range(8):
            P.mm(bk[:, 64 + n * 4:64 + (n + 1) * 4], SL_bf, ghi[:, n, :], start=True, stop=False)
            P.mm(bk[:, 64 + n * 4:64 + (n + 1) * 4], SL_bf, glo[:, n, :], start=False, stop=True)
        P.copy("dve", gc[:], bk[:, 0:32].rearrange("p (n c) -> p n c", c=4))
        P.copy("dve", gmg[:], bk[:, 64:96].rearrange("p (n c) -> p n c", c=4))
        P.act(egc[:], gc[:], AF.Exp)
        P.tt("dve", bge[:], beta[:], egc[:], ALU.mult)
        P.act(ekt[:], gmg[:], AF.Exp)
        P.copy("dve", gchi[:], gc[:])
        P.tt("dve", gclo[:], gc[:], gchi[:], ALU.subtract)
        add_dump("gc%d" % hf, gc[:], [128, 8, 4])
        add_dump("beta%d" % hf, beta[:], [128, 8, 4])
        LNS = float(np.log(DK_SCALE))
        ident_bc = ident_bf.unsqueeze(1).to_broadcast([128, 4, 128])
        MiT_bc = MiT[:].unsqueeze(1).to_broadcast([128, 4, 128])
        Ms_bc = Ms.unsqueeze(1).to_broadcast([128, 4, 128])
        slots = []
        NSLOT = 3
        for sl in range(NSLOT):
            def f4(name, dt=BF16, sl=sl):
                return P.sb("g%d_%s" % (sl, name), [128, 4, 128], dt)
            d = dict(sl=sl)
            for n_ in ("eT", "eE", "EGb", "u32"):
                d[n_] = f4(n_, F32)
            d["stmp"] = slots[0]["stmp"] if slots else f4("stmp", F32)
            for n_ in ("N", "M", "N2", "M2", "N3", "M3", "XT", "Rk", "Rv", "ktl", "attT", "qdT",
                       "wT", "vnb"):
                d[n_] = f4(n_)
            d["dhi"], d["dlo"] = d["N2"], d["M2"]
            d["cdc"] = P.sb("g%d_cd" % sl, [128, 4], F32)
            slots.append(d)

        def gdn_chunk(n, B):
            cs = slice(n * 128, (n + 1) * 128)
            eT, eE, EGb, u32, stmp, dhi, dlo = (B[k] for k in ("eT", "eE", "EGb", "u32", "stmp", "dhi", "dlo"))
            Nb, Mb, XT, Rk, Rv, ktl, attT, qdT, wT, vnb, cdc = (B[k] for k in
                ("N", "M", "XT", "Rk", "Rv", "ktl", "attT", "qdT", "wT", "vnb", "cdc"))
            P.tt("pool", dhi[:], ident_bc, bc4(gchi[:, n, :]), ALU.mult)
            P.tt("pool", dlo[:], ident_bc, bc4(gclo[:, n, :]), ALU.mult)
            yield
            gb = nb()
            P.mm(gb[:], ones_bf[:], dhi[:].rearrange("p h i -> p (h i)"), start=True, stop=False)
            P.mm(gb[:], ones_bf[:], dlo[:].rearrange("p h i -> p (h i)"), start=False, stop=True)
            gbv = v4(gb)
            gcb = bc4(gc[:, n, :])
            P.tt("dve", eT[:], gbv, gcb, ALU.min)
            P.tt("dve", eE[:], gbv, gcb, ALU.max)
            P.act(EGb[:], gbv, AF.Exp, bias=LNS)
            P.act(cdc[:].unsqueeze(2), gbv[:, :, 127:128], AF.Exp)
            b4 = nb()
            for h in range(4):
                P.tr(v4b(b4)[:, h, :], kd[:, h, cs], ident_bf)
            P.tt("dve", Rk[:], v4b(b4), bc4(bge[:, n, :]), ALU.mult)
            P.tt("dve", ktl[:], v4b(b4), bc4(ekt[:, n, :]), ALU.mult)
            yield
            P.tt("pool", eT[:], eT[:], gcb, ALU.subtract)
            P.tt("pool", eE[:], eE[:], gcb, ALU.subtract)
            b5 = nb()
            for h in range(4):
                P.tr(v4b(b5)[:, h, :], vd[:, h, cs], ident_bf)
            P.tt("dve", Rv[:], v4b(b5), bc4(beta[:, n, :]), ALU.mult)
            yield
            P.act(eT[:], eT[:], AF.Exp)
            P.act(eE[:], eE[:], AF.Exp, scale=-1.0)
            P.tt("pool", qdT[:], qd[:, :, cs], EGb[:], ALU.mult)
            yield
            P.tt("pool", eT[:], eT[:], MiT_bc, ALU.mult)
            P.tt("pool", eE[:], eE[:], Ms_bc, ALU.mult)
            P.tt("pool", eE[:], eE[:], bc4(nbeta[:, n, :]), ALU.mult)
            yield
            b1 = nb()
            for h in range(4):
                P.mm(v4(b1)[:, h, :], kd[:, h, cs], kd[:, h, cs])
            b2 = nb()
            for h in range(4):
                P.mm(v4(b2)[:, h, :], kd[:, h, cs], qd[:, h, cs])
            P.tt("dve", Nb[:], v4(b1), eE[:], ALU.mult)
            P.tt("dve", attT[:], v4(b2), eT[:], ALU.mult)
            yield
            b3 = nb()
            for h in range(4):
                P.tr(v4b(b3)[:, h, :], Nb[:, h, :], ident_bf)
            P.copy("act", Mb[:], v4b(b3))
            yield
            P.tt("pool", XT[:], Mb[:], ident_bc, ALU.add)
            Nc, Mc = Nb, Mb
            pairs = [(B["N2"], B["M2"]), (B["N3"], B["M3"])]
            for k in range(6):
                Nn, Mn = pairs[k % 2]
                ba = nb()
                for h in range(4):
                    P.mm(v4(ba)[:, h, :], Mc[:, h, :], Nc[:, h, :])
                if k < 5:
                    bb = nb()
                    for h in range(4):
                        P.mm(v4(bb)[:, h, :], Nc[:, h, :], Mc[:, h, :])
                P.copy("act", Nn[:], v4(ba))
                if k < 5:
                    P.copy("dve", Mn[:], v4(bb))
                yield
                bx = nb()
                for h in range(4):
                    P.mm(v4(bx)[:, h, :], Nn[:, h, :], XT[:, h, :])
                P.tt("dve", XT[:], XT[:], v4(bx), ALU.add)
                yield
                Nc, Mc = Nn, Mn
            b6 = nb()
            for h in range(4):
                P.mm(v4(b6)[:, h, :], XT[:, h, :], Rv[:, h, :])
            b7 = nb()
            for h in range(4):
                P.mm(v4(b7)[:, h, :], Rk[:, h, :], XT[:, h, :])
            P.copy("act", u32[:], v4(b6))
            P.copy("act", wT[:], v4(b7))
            yield
            b8 = nb()
            for h in range(4):
                P.mm(v4(b8)[:, h, :], wT[:, h, :], Sgb[:, h, :])
            P.tt("dve", vnb[:], u32[:], v4(b8), ALU.subtract)
            yield
            b9 = nb()
            for h in range(4):
                P.mm(v4(b9)[:, h, :], ktl[:, h, :], vnb[:, h, :])
            bo = nb(hold=True)
            for h in range(4):
                P.mm(v4(bo)[:, h, :], Sgb[:, h, :], qdT[:, h, :], start=True, stop=False)
                P.mm(v4(bo)[:, h, :], vnb[:, h, :], attT[:, h, :], start=False, stop=True)
            P.tt("pool", stmp[:], Sg32[:], bc4(cdc[:]), ALU.mult)
            P.tt("dve", Sg32[:], stmp[:], v4(b9), ALU.add)
            P.copy("act", Sgb[:], Sg32[:])
            yield
            ntmp = (dhi[:].rearrange("p h i -> p (h i)"), eE[:].rearrange("p h i -> p (h i)"),
                    eT[:].rearrange("p h i -> p (h i)"))
            yield from out_norm_g(bo, pv[:, PV_GOG:PV_GOG + 1], gd[:, :, cs], yb[:, 4:8, cs], ntmp)

        active = []
        NSTAGE = 27
        nstart = 0
        step = 0
        while nstart < 8 or active:
            if nstart < 8 and len(active) < NSLOT and step >= nstart * (NSTAGE // NSLOT):
                active.append(gdn_chunk(nstart, slots[nstart % NSLOT]))
                nstart += 1
            run_pipe(active)
            step += 1
        add_dump("yd%d" % hf, yb[:, 4:8, :], [128, 4, TH])
        P.sb_top = m0
        if stage < 3:
            return
        phase('m0_outproj%d' % hf)
        wb = WBuf("c", cast="dve", size=2048)
        out_proj(wb, w_out0.ap(), hf, "wout0")
        P.sb_top = m0

    def ffn(l, hf, inject=None):
        phase('ffn%d_up%d' % (l, hf))
        rmsnorm_half(hf, PV_NF + l * 8)
        m0 = P.sb_top
        mb = P.sb("f_m", [128, 22, TH], BF16)
        wb = WBuf("f")
        stg = [[P.sb("f_st%d_%d" % (ci, i), [128, 514], F32) for i in range(2)] for ci in range(2)]
        NBUF = 3
        yb_at = P.base[yb.name][1]
        accg = [P.sb("f_ag%d" % i, [128, 512], F32, at=yb_at + i * 2048) for i in range(NBUF)]
        accv = [P.sb("f_av%d" % i, [128, 512], F32, at=yb_at + (NBUF + i) * 2048) for i in range(NBUF)]
        wup = f_up.ap()[l]
        wdn = f_dn.ap()[l]
        dspec = [([kview(wdn[:, dc * 128:(dc + 1) * 128])], "dn%d_%d" % (l, dc)) for dc in range(8)]
        cw, cbias = PV_FCW + l * 132, PV_FCB + l * 44
        cnt = [0]

        def cons(bi, ci, tb, bk):
            g = bi
            cc = g if ci == 0 else 22 + g
            i = cnt[0] % NBUF
            if ci == 1:
                cnt[0] += 1
            sg = stg[ci][tb % 2]
            if tb == 0:
                if hf == 0:
                    P.memset("pool", sg[:, 0:2], 0.0)
                else:
                    P.copy("pool", sg[:, 0:2], fhalo[:, cc, :])
                hnext = stg[ci][1][:, 0:2]
            else:
                hnext = fhalo[:, cc, :] if hf == 0 else None
            acc = accg[i] if ci == 0 else accv[i]
            conv_block(bk, sg, 3, lambda j: pv[:, cw + j * 44 + cc:cw + j * 44 + cc + 1],
                       pv[:, cbias + cc:cbias + cc + 1], acc[:], hnext)
            if ci == 0:
                return
            yield
            yield
            P.act(accg[i][:], accg[i][:], AF.Silu)
            P.tt("pool", mb[:, g, tb * 512:(tb + 1) * 512], accg[i][:], accv[i][:], ALU.mult)
        proj_fm(wb, [([kview(wup[:, g * 128:(g + 1) * 128]), kview(wup[:, DFF + g * 128:DFF + (g + 1) * 128])],
                      "up%d_%d" % (l, g)) for g in range(22)], 8,
                lambda kc, tb: hn[:, kc, tb * 512:(tb + 1) * 512], cons, hf, after=(dspec[0], 22))
        phase('ffn%d_down%d' % (l, hf))
        pipe = inject(store_slots(2, ot_at=P.base[stg[0][0].name][1])) if inject is not None else None
        nxt = wb.get(dspec[0], 22, hf)
        for dc in range(8):
            bfv = nxt
            if dc < 7:
                nxt = wb.load(dspec[dc + 1][0], 22, dspec[dc + 1][1], hf)
            for tb in range(2):
                bk = nb()
                for g in range(22):
                    P.mm(bk[:], bfv[:, g, :], mb[:, g, tb * 512:(tb + 1) * 512], start=(g == 0), stop=(g == 21))
                ts_ = slice(hf * TH + tb * 512, hf * TH + (tb + 1) * 512)
                P.tt("dve", hT[:, dc, ts_], hT[:, dc, ts_], bk[:], ALU.add)
                if pipe is not None:
                    pipe.step()
        while pipe is not None and not pipe.done():
            pipe.step()
        P.sb_top = m0

    def mixer1(hf):
        phase('m1_%d' % hf)
        rmsnorm_half(hf, PV_NM + 8)
        m0 = P.sb_top
        wb = WBuf("l", cast="dve", size=2048)
        wab = P.sb("l_wa", [128, 8, 128], BF16)
        wxb = P.sb("l_wx", [128, 8, 128], BF16)
        w1 = lw_in.ap()
        lspec = [([kview(w1[:, 1024 + c * 128:1024 + (c + 1) * 128]), kview(w1[:, c * 128:(c + 1) * 128])],
                  "lwin_%d" % c) for c in range(8)]
        ospec = lambda wd, key: [([kview(wd[:, b * 256:(b + 1) * 256])], "%s_%d" % (key, b)) for b in range(4)]
        bfa = wb.load([lw_a.ap().rearrange("n i j -> i n j")], 8, "lwa", hf)
        bfx = wb.load([lw_x.ap().rearrange("n i j -> i n j")], 8, "lwx", hf)
        P.copy("pool", wab[:], bfa)
        wb.prefetch(lspec[0], 8, hf)
        P.copy("pool", wxb[:], bfx)
        stg = [P.sb("l_st%d" % i, [128, 515], F32) for i in range(2)]
        NBX, NBG = 4, 4

        def bufs(name, dt=F32, n=NBX):
            return [P.sb("l_%s%d" % (name, i), [128, 512], dt) for i in range(n)]
        xc, ra, igu, a2, hs = (bufs(n_) for n_ in ("xc", "ra", "igu", "a2", "hs"))
        xcb = bufs("xcb", BF16)
        xg, x2z = bufs("xg", F32, NBG), bufs("x2z", F32, NBG)
        cnt = [0, 0]
        hs_of = {}

        def cons(bi, ci, tb, bk):
            c = bi
            tsl = slice(tb * 512, (tb + 1) * 512)
            if ci == 0:
                i = cnt[0] % NBX
                cnt[0] += 1
                hs_of[(c, tb)] = hs[i]
                sg = stg[tb % 2]
                W = 4
                wcol = lambda j: pv[:, PV_LCW + j * 8 + c:PV_LCW + j * 8 + c + 1]
                if tb == 0:
                    if hf == 0:
                        P.memset("pool", sg[:, 0:3], 0.0)
                    else:
                        P.copy("pool", sg[:, 0:3], lhalo[:, c, :])
                    hnext = stg[1][:, 0:3]
                else:
                    hnext = lhalo[:, c, :] if hf == 0 else None
                P.copy("act", sg[:, W - 1:W - 1 + 512], bk[:])
                P.act(xc[i][:], bk[:], AF.Identity, scale=wcol(W - 1), bias=pv[:, PV_LCB + c:PV_LCB + c + 1])
                yield
                for j in range(W - 1):
                    P.stt("dve", xc[i][:], sg[:, j:j + 512], wcol(j), xc[i][:], ALU.mult, ALU.add)
                P.copy("dve", xcb[i][:], xc[i][:])
                if hnext is not None:
                    P.copy("pool", hnext, sg[:, 512:512 + W - 1])
                yield
                b1 = nb(hold=True)
                P.mm(b1[:], wab[:, c, :], xcb[i][:])
                b2 = nb(hold=True)
                P.mm(b2[:], wxb[:, c, :], xcb[i][:])
                yield
                P.act(ra[i][:], b1[:], AF.Exp, scale=-1.0, bias=nlb[:, c:c + 1])
                P.act(igu[i][:], b2[:], AF.Exp, scale=-1.0, bias=nlb[:, 8 + c:9 + c])
                rel(b1)
                rel(b2)
                P.act(ra[i][:], ra[i][:], AF.Ln, bias=1.0)
                P.act(igu[i][:], igu[i][:], AF.Ln, bias=1.0)
                P.act(ra[i][:], ra[i][:], AF.Exp, scale=-1.0)
                P.act(igu[i][:], igu[i][:], AF.Exp, scale=-1.0)
                P.act(ra[i][:], ra[i][:], AF.Exp, scale=lsc[:, c:c + 1])
                P.act(a2[i][:], ra[i][:], AF.Square)
                P.act(a2[i][:], a2[i][:], AF.Ln, scale=-1.0, bias=1.0000001)
                P.act(a2[i][:], a2[i][:], AF.Exp, scale=0.5)
                yield
                P.tt("dve", igu[i][:], igu[i][:], xc[i][:], ALU.mult)
                P.tt("dve", igu[i][:], igu[i][:], a2[i][:], ALU.mult)
                init = 0.0 if (hf == 0 and tb == 0) else lstate[:, c:c + 1]
                rd = [ra[i][:], igu[i][:]] + ([] if isinstance(init, float) else [init])
                o_, a_, u_ = hs[i][:], ra[i][:], igu[i][:]
                P.add("dve", lambda e, o_=o_, a_=a_, u_=u_, init=init: e.tensor_tensor_scan(
                    o_, a_, u_, init, ALU.mult, ALU.add), rd, [o_])
                P.copy("dve", lstate[:, c:c + 1], hs[i][:, 511:512])
            else:
                i = cnt[1] % NBG
                cnt[1] += 1
                hsb = hs_of[(c, tb)]
                P.copy("act", xg[i][:], bk[:])
                yield
                P.tt("pool", x2z[i][:], xg[i][:], xg[i][:], ALU.mult)
                yield
                P.ts("dve", x2z[i][:], x2z[i][:], 0.044715, ALU.mult, 1.0, ALU.add)
                P.tt("dve", x2z[i][:], x2z[i][:], xg[i][:], ALU.mult)
                yield
                P.act(x2z[i][:], x2z[i][:], AF.Exp, scale=-1.5957691216057308)
                P.act(x2z[i][:], x2z[i][:], AF.Ln, bias=1.0)
                P.act(x2z[i][:], x2z[i][:], AF.Exp, scale=-1.0)
                yield
                P.tt("pool", x2z[i][:], x2z[i][:], xg[i][:], ALU.mult)
                yield
                P.tt("pool", yb[:, c, tsl], x2z[i][:], hsb[:], ALU.mult)
        proj_fm(wb, lspec, 8, lambda kc, tb: hn[:, kc, tb * 512:(tb + 1) * 512], cons, hf,
                after=(ospec(lw_out.ap(), "lwout")[0], 8))
        add_dump("yl%d" % hf, yb[:], [128, 8, TH])
        out_proj(wb, lw_out.ap(), hf, "lwout")
        P.sb_top = m0

    last = []

    class Pipe:
        def __init__(self, factories, nflight, gap):
            self.fac = list(factories)
            self.nflight, self.gap = nflight, gap
            self.active, self.k, self.stepno = [], 0, 0

        def done(self):
            return self.k >= len(self.fac) and not self.active

        def step(self):
            if self.k < len(self.fac) and len(self.active) < self.nflight and self.stepno >= self.k * self.gap:
                self.active.append(self.fac[self.k](self.k % self.nflight))
                self.k += 1
            run_pipe(self.active)
            self.stepno += 1

    def store_slots(n, ot_at=None):
        return [dict(sq=P.sb("so_sq%d" % i, [128, KC, 128], BF16), rstd=P.sb("so_r%d" % i, [128, 128], F32),
                     ot=P.sb("so_ot%d" % i, [128, D], F32, at=None if ot_at is None else ot_at + i * 4096))
                for i in range(n)]

    def store_chunk(tcg, B):
        tsl = slice(tcg * 128, (tcg + 1) * 128)
        hv_ = hT[:, :, tsl]
        P.act(B["sq"][:], hv_, AF.Square)
        yield
        bk = nb(hold=True)
        for c in range(KC):
            P.mm(bk[:, 0:128], ones_bf[:], B["sq"][:, c, :], start=(c == 0), stop=(c == KC - 1))
        yield
        P.act(B["rstd"][:], bk[:, 0:128], AF.Ln, scale=1.0 / D, bias=EPS)
        rel(bk)
        P.act(B["rstd"][:], B["rstd"][:], AF.Exp, scale=-0.5)
        yield
        P.tt("dve", hv_, hv_, B["rstd"][:].unsqueeze(1).to_broadcast([128, KC, 128]), ALU.mult)
        P.tt("dve", hv_, hv_, pv[:, PV_NFIN:PV_NFIN + KC].unsqueeze(2).to_broadcast([128, KC, 128]), ALU.mult)
        yield
        for g in range(2):
            bk2 = nb()
            for j in range(4):
                P.tr(bk2[:, j * 128:(j + 1) * 128], hT[:, g * 4 + j, tsl], ident)
            P.copy("dve" if g == 0 else "act", B["ot"][:, g * 512:(g + 1) * 512], bk2[:])
        last.append(P.dma(out_d.ap()[tcg * 128:(tcg + 1) * 128, :], B["ot"][:]))

    def store_pipe(hf, slots, gap=2):
        return Pipe([(lambda si, tcg=hf * 8 + k: store_chunk(tcg, slots[si])) for k in range(8)], len(slots), gap)

    def store(hf, norm):
        phase('store%d' % hf)
        m0 = P.sb_top
        o32 = P.sb("o32", [128, KC, 512], F32)
        ot = [P.sb("ot%d" % i, [128, D], F32) for i in range(2)]
        sq = P.sb("fn_sq", [128, KC, 512], BF16)
        rstd = P.sb("fn_rstd", [128, 512], F32)
        for tb in range(2):
            ts_ = slice(hf * TH + tb * 512, hf * TH + (tb + 1) * 512)
            if norm:
                for c in range(KC):
                    if c % 2 == 0:
                        P.act(sq[:, c, :], hT[:, c, ts_], AF.Square)
                    else:
                        P.tt("pool", sq[:, c, :], hT[:, c, ts_], hT[:, c, ts_], ALU.mult)
                bk = nb()
                for c in range(KC):
                    P.mm(bk[:], ones_bf[:], sq[:, c, :], start=(c == 0), stop=(c == KC - 1))
                P.act(rstd[:], bk[:], AF.Ln, scale=1.0 / D, bias=EPS)
                P.act(rstd[:], rstd[:], AF.Exp, scale=-0.5)
                for c in range(KC):
                    P.stt("dve", o32[:, c, :], hT[:, c, ts_], pv[:, PV_NFIN + c:PV_NFIN + c + 1], rstd[:],
                          ALU.mult, ALU.mult)
            for t4 in range(4):
                tcg = hf * 8 + tb * 4 + t4
                otb = ot[tcg % 2]
                for g in range(2):
                    bk = nb()
                    for j in range(4):
                        c = g * 4 + j
                        src = o32[:, c, t4 * 128:(t4 + 1) * 128] if norm else \
                            hT[:, c, tcg * 128:(tcg + 1) * 128]
                        P.tr(bk[:, j * 128:(j + 1) * 128], src, ident)
                    P.copy("dve" if g == 0 else "act", otb[:, g * 512:(g + 1) * 512], bk[:])
                last.append(P.dma(out_d.ap()[tcg * 128:(tcg + 1) * 128, :], otb[:]))
        P.sb_top = m0

    P.sb_top = SCR
    for hf in range(2):
        if stage >= 1:
            mixer0(hf)
        if stage >= 4:
            ffn(0, hf)
    if stage >= 7:
        mixer1(0)
        ffn(1, 0)
        mixer1(1)
        ffn(1, 1, inject=lambda slots: store_pipe(0, slots))
        phase('store1')
        m0_ = P.sb_top
        pipe = store_pipe(1, store_slots(6), gap=1)
        while not pipe.done():
            pipe.step()
        P.sb_top = m0_
    else:
        for hf in range(2):
            if stage >= 5:
                mixer1(hf)
            if stage >= 6:
                ffn(1, hf)
        for hf in range(2):
            store(hf, False)
    P.emit(final_wait_ops=last + list(dump_d.values()))
    return nc, P


_CACHE = {}


def prep_inputs(inp):
    cmat, rope = host_consts()
    pvec, hvec = host_pvec(inp)
    shared = dict(
        w_in0=np.ascontiguousarray(inp["ret_gdn_w_in"][0]), w_out0=np.ascontiguousarray(inp["ret_gdn_w_out"][0]),
        lw_in=np.ascontiguousarray(inp["lru_w_in"][0]), lw_a=np.ascontiguousarray(inp["lru_w_a"][0]),
        lw_x=np.ascontiguousarray(inp["lru_w_x"][0]), lw_out=np.ascontiguousarray(inp["lru_w_out"][0]),
        f_up=np.ascontiguousarray(inp["ffn_w_up"]), f_dn=np.ascontiguousarray(inp["ffn_w_down"]),
        pvec=pvec, hvec=hvec, cmat=cmat, rope=rope)
    return shared


def kernel(**inputs):
    inp = {k: np.asarray(v, dtype=np.float32) for k, v in inputs.items()}
    if "nc" not in _CACHE:
        _CACHE["nc"] = build()[0]
    nc = _CACHE["nc"]
    shared = prep_inputs(inp)
    n = 8
    in_maps = [dict(shared, x=np.ascontiguousarray(inp["x"][b])) for b in range(n)]
    res = run_bass_kernel_spmd(nc, in_maps, core_ids=list(range(n)))
    return np.stack([np.asarray(r["out"], dtype=np.float32) for r in res.results], axis=0)
```
